# Optimizing a Trainium2 kernel written in Bass

```python
import jax, jax.numpy as jnp
from jax import lax
import numpy as np

D_MODEL = 1024
BATCH = 8
SEQ = 2048
DEPTH = 2
DEC_BATCH = 128
DEC_SEQ = 8
PAST_LEN = 2048
PAGE_SIZE = 128

HEAD_DIM = 64
N_A = 4
N_B = 6
N_C = 6
W_A = N_A * HEAD_DIM
W_B = N_B * HEAD_DIM
W_C = N_C * HEAD_DIM
MIX = W_A + W_B + W_C
CHUNK = 128
Q_BLOCK = 128
D_FF = 256 * (-(-8 * D_MODEL // (3 * 256)))
N_IN = 2 * W_A + 3 * W_B + 3 * W_C + N_C
EPS = 1e-6
FORGET_BIAS = 2.0
NEG = -1e30

kernel_name = "hybrid_chunkmlp_stickbreak_fox_decode_step"


def rms_norm(x, g):
    xf = x.astype(jnp.float32)
    y = xf * lax.rsqrt(jnp.mean(xf * xf, axis=-1, keepdims=True) + EPS)
    return (y * g.astype(jnp.float32)).astype(x.dtype)


def split_proj(z, b_f):
    lead = z.shape[:-1]
    sizes = [W_A, W_A, W_B, W_B, W_B, W_C, W_C, W_C, N_C]
    u_a, v_a, q_b, k_b, v_b, q_c, k_c, v_c, f_c = jnp.split(z, np.cumsum(sizes)[:-1].tolist(), axis=-1)
    hb = lambda a: a.reshape(*lead, N_B, HEAD_DIM)
    hc = lambda a: a.reshape(*lead, N_C, HEAD_DIM)
    log_f = jax.nn.log_sigmoid(f_c.astype(jnp.float32) + b_f.astype(jnp.float32))
    return (jax.nn.gelu(u_a), jax.nn.gelu(v_a), hb(q_b), hb(k_b), hb(v_b),
            hc(q_c), hc(k_c), hc(v_c), log_f)


def chunk_spatial_gate(u, v_n, w_s, b_s):
    B, L, _ = u.shape
    n = min(L, CHUNK)
    nc = L // n
    vc = v_n.reshape(B, nc, n, N_A, HEAD_DIM).astype(jnp.float32)
    ws = jnp.where(jnp.tril(jnp.ones((n, n), dtype=bool)), w_s[:, :n, :n].astype(jnp.float32), 0.0)
    mixed = jnp.einsum('gts,bcsgd->bctgd', ws, vc) + b_s[:, :n].astype(jnp.float32).T[None, None, :, :, None]
    out = u.reshape(B, nc, n, N_A, HEAD_DIM) * mixed.astype(u.dtype)
    return out.reshape(B, L, W_A)


def stick_breaking(q, k, v, q_pos, k_pos):
    z = jnp.einsum('bthd,blhd->bhtl', q.astype(jnp.float32), k.astype(jnp.float32)) * (HEAD_DIM ** -0.5)
    mask = k_pos[None, :] < q_pos[:, None]
    log_1m = jnp.where(mask, jax.nn.log_sigmoid(-z), 0.0)
    after = lax.cumsum(log_1m, axis=3, reverse=True) - log_1m
    a = jnp.where(mask, jnp.exp(jax.nn.log_sigmoid(z) + after), 0.0)
    return jnp.einsum('bhtl,blhd->bthd', a, v.astype(jnp.float32)).astype(v.dtype)


def forgetting_attention(q, k, v, fq, fk, q_pos, k_pos):
    z = jnp.einsum('bthd,blhd->bhtl', q.astype(jnp.float32), k.astype(jnp.float32)) * (HEAD_DIM ** -0.5)
    z = z + jnp.transpose(fq, (0, 2, 1))[:, :, :, None] - jnp.transpose(fk, (0, 2, 1))[:, :, None, :]
    mask = k_pos[None, :] <= q_pos[:, None]
    p = jax.nn.softmax(jnp.where(mask, z, NEG), axis=-1)
    return jnp.einsum('bhtl,blhd->bthd', p, v.astype(jnp.float32)).astype(v.dtype)


def sb_prompt(q, k, v):
    B, S = q.shape[:2]
    nb = S // Q_BLOCK
    pos = jnp.arange(S, dtype=jnp.int32)
    qb = q.reshape(B, nb, Q_BLOCK, N_B, HEAD_DIM).swapaxes(0, 1)
    pb = pos.reshape(nb, Q_BLOCK)
    out = lax.map(lambda a: stick_breaking(a[0], k, v, a[1], pos), (qb, pb))
    return out.swapaxes(0, 1).reshape(B, S, N_B, HEAD_DIM)


def fox_prompt(q, k, v, log_f):
    B, S = q.shape[:2]
    nb = S // Q_BLOCK
    F = jnp.cumsum(log_f, axis=1)
    pos = jnp.arange(S, dtype=jnp.int32)
    qb = q.reshape(B, nb, Q_BLOCK, N_C, HEAD_DIM).swapaxes(0, 1)
    fb = F.reshape(B, nb, Q_BLOCK, N_C).swapaxes(0, 1)
    pb = pos.reshape(nb, Q_BLOCK)
    out = lax.map(lambda a: forgetting_attention(a[0], k, v, a[1], F, a[2], pos), (qb, fb, pb))
    return out.swapaxes(0, 1).reshape(B, S, N_C, HEAD_DIM)


def gather_pages(cache, page_table):
    g = cache[page_table]
    return g.reshape(g.shape[0], g.shape[1] * g.shape[2], *g.shape[3:])


def merge_groups(a_out, b_out, c_out, g_mix, w_o):
    lead = a_out.shape[:-1]
    cat = jnp.concatenate([rms_norm(a_out, g_mix[:W_A]),
                           rms_norm(b_out.reshape(*lead, W_B), g_mix[W_A:W_A + W_B]),
                           rms_norm(c_out.reshape(*lead, W_C), g_mix[W_A + W_B:])], axis=-1)
    return cat @ w_o


def swiglu(x, w_in, w_out):
    h = x @ w_in
    return (jax.nn.silu(h[..., :D_FF]) * h[..., D_FF:]) @ w_out


def setup_inputs(seed: int = 0) -> dict:
    key = jax.random.key(seed)
    ks = jax.random.split(key, 24)
    n_pages = PAST_LEN // PAGE_SIZE
    n_pool = (DEC_BATCH * n_pages * 5) // 4
    f32 = jnp.float32
    nrm = lambda k, s, sc=1.0: jax.random.normal(k, s, f32) * sc
    kv_shape_b = (DEPTH, n_pool, PAGE_SIZE, N_B, HEAD_DIM)
    kv_shape_c = (DEPTH, n_pool, PAGE_SIZE, N_C, HEAD_DIM)
    perm = jax.random.permutation(ks[7], n_pool)[:DEC_BATCH * n_pages]
    return {
        "x_prompt": nrm(ks[0], (BATCH, SEQ, D_MODEL)),
        "x_sample": nrm(ks[1], (DEC_BATCH, DEC_SEQ, D_MODEL)),
        "cache_sb_k": nrm(ks[2], kv_shape_b),
        "cache_sb_v": nrm(ks[3], kv_shape_b),
        "cache_fox_k": nrm(ks[4], kv_shape_c),
        "cache_fox_v": nrm(ks[5], kv_shape_c),
        "cache_fox_logf": jax.nn.log_sigmoid(FORGET_BIAS + nrm(ks[6], (DEPTH, n_pool, PAGE_SIZE, N_C))),
        "page_table": perm.reshape(DEC_BATCH, n_pages).astype(jnp.int32),
        "g_attn": 1.0 + nrm(ks[8], (DEPTH, D_MODEL), 0.02),
        "w_in": nrm(ks[9], (DEPTH, D_MODEL, N_IN), D_MODEL ** -0.5),
        "b_f": FORGET_BIAS + nrm(ks[10], (DEPTH, N_C), 0.1),
        "g_v": 1.0 + nrm(ks[11], (DEPTH, W_A), 0.02),
        "w_s": nrm(ks[12], (DEPTH, N_A, CHUNK, CHUNK), CHUNK ** -0.5),
        "b_s": 1.0 + nrm(ks[13], (DEPTH, N_A, CHUNK), 0.1),
        "g_mix": 1.0 + nrm(ks[14], (DEPTH, MIX), 0.02),
        "w_o": nrm(ks[15], (DEPTH, MIX, D_MODEL), MIX ** -0.5),
        "g_ffn": 1.0 + nrm(ks[16], (DEPTH, D_MODEL), 0.02),
        "w_ffn_in": nrm(ks[17], (DEPTH, D_MODEL, 2 * D_FF), D_MODEL ** -0.5),
        "w_ffn_out": nrm(ks[18], (DEPTH, D_FF, D_MODEL), D_FF ** -0.5),
        "g_final": 1.0 + nrm(ks[19], (D_MODEL,), 0.02),
    }


def reference(x_prompt, x_sample, cache_sb_k, cache_sb_v, cache_fox_k, cache_fox_v, cache_fox_logf,
              page_table, g_attn, w_in, b_f, g_v, w_s, b_s, g_mix, w_o, g_ffn, w_ffn_in, w_ffn_out,
              g_final):
    past = page_table.shape[1] * PAGE_SIZE
    T = x_sample.shape[1]
    q_pos = past + jnp.arange(T, dtype=jnp.int32)
    k_pos = jnp.arange(past + T, dtype=jnp.int32)
    yp, ys = x_prompt, x_sample
    p_sbk, p_sbv, p_fk, p_fv, p_lf = [], [], [], [], []
    s_sbk, s_sbv, s_fk, s_fv, s_lf, s_cv = [], [], [], [], [], []
    for l in range(DEPTH):
        u, va, qb, kb, vb, qc, kc, vc, lf = split_proj(rms_norm(yp, g_attn[l]) @ w_in[l], b_f[l])
        va = rms_norm(va, g_v[l])
        a_out = chunk_spatial_gate(u, va, w_s[l], b_s[l])
        b_out = sb_prompt(qb, kb, vb)
        c_out = fox_prompt(qc, kc, vc, lf)
        yp = yp + merge_groups(a_out, b_out, c_out, g_mix[l], w_o[l])
        yp = yp + swiglu(rms_norm(yp, g_ffn[l]), w_ffn_in[l], w_ffn_out[l])
        p_sbk.append(kb); p_sbv.append(vb); p_fk.append(kc); p_fv.append(vc); p_lf.append(lf)
        u, va, qb, kb, vb, qc, kc, vc, lf = split_proj(rms_norm(ys, g_attn[l]) @ w_in[l], b_f[l])
        va = rms_norm(va, g_v[l])
        a_out = chunk_spatial_gate(u, va, w_s[l], b_s[l])
        kb_all = jnp.concatenate([gather_pages(cache_sb_k[l], page_table), kb], axis=1)
        vb_all = jnp.concatenate([gather_pages(cache_sb_v[l], page_table), vb], axis=1)
        b_out = stick_breaking(qb, kb_all, vb_all, q_pos, k_pos)
        kc_all = jnp.concatenate([gather_pages(cache_fox_k[l], page_table), kc], axis=1)
        vc_all = jnp.concatenate([gather_pages(cache_fox_v[l], page_table), vc], axis=1)
        lf_all = jnp.concatenate([gather_pages(cache_fox_logf[l], page_table).astype(jnp.float32), lf], axis=1)
        F = jnp.cumsum(lf_all, axis=1)
        c_out = forgetting_attention(qc, kc_all, vc_all, F[:, past:], F, q_pos, k_pos)
        ys = ys + merge_groups(a_out, b_out, c_out, g_mix[l], w_o[l])
        ys = ys + swiglu(rms_norm(ys, g_ffn[l]), w_ffn_in[l], w_ffn_out[l])
        s_sbk.append(kb); s_sbv.append(vb); s_fk.append(kc); s_fv.append(vc); s_lf.append(lf); s_cv.append(va)
    y_prompt = rms_norm(yp, g_final)
    y_sample = rms_norm(ys, g_final)
    return (y_prompt, y_sample,
            jnp.stack(p_sbk), jnp.stack(p_sbv), jnp.stack(p_fk), jnp.stack(p_fv), jnp.stack(p_lf),
            jnp.stack(s_sbk), jnp.stack(s_sbv), jnp.stack(s_fk), jnp.stack(s_fv), jnp.stack(s_lf),
            jnp.stack(s_cv))
```

```python
import numpy as np
import ml_dtypes
from contextlib import ExitStack
import concourse.bass as bass
import concourse.mybir as mybir
from concourse.bass_utils import run_bass_kernel_spmd

F32 = mybir.dt.float32
BF16 = mybir.dt.bfloat16
I32 = mybir.dt.int32
AF = mybir.ActivationFunctionType
ALU = mybir.AluOpType

ENGS = ("pe", "act", "dve", "pool", "sp")
EPOCH = 6000
NDSEM = 12

D = 1024
S = 2048
NT = 16
GT = 4
NG = NT // GT
NIN = 2822
DFF = 2816
NJ = 22
L = 2
NPOOL = 2560
NEGM = -30000.0


class Buf:
    __slots__ = ("name", "t", "lw", "rd")

    def __init__(self, name, t):
        self.name = name
        self.t = t
        self.lw = None
        self.rd = []

    def __getitem__(self, k):
        return self.t[k]


class Prog:
    def __init__(self, nc, stack):
        self.nc = nc
        self.stack = stack
        self.ops = {e: [] for e in ENGS}
        self.seen_c = {e: {e2: -1 for e2 in ENGS} for e in ENGS}
        self.seen_d = {e: set() for e in ENGS}
        self.ndma = {e: 0 for e in ENGS}
        self.dtok = {e: [] for e in ENGS}
        self.names = set()

    def sb(self, name, shape, dt):
        assert name not in self.names, name
        self.names.add(name)
        t = self.stack.enter_context(self.nc.sbuf_tensor(name, list(shape), dt))
        return Buf(name, t)

    def ps(self, name, shape, dt):
        t = self.stack.enter_context(self.nc.psum_tensor(name, list(shape), dt))
        return Buf(name, t)

    def op(self, eng, fn, r=(), w=(), dma=False):
        ops = self.ops[eng]
        idx = len(ops)
        deps = set()
        for b in r:
            if b.lw is not None:
                deps.add(b.lw)
        for b in w:
            if b.lw is not None:
                deps.add(b.lw)
            for t in b.rd:
                deps.add(t)
        waits = []
        if dma:
            d = self.ndma[eng]
            self.ndma[eng] += 1
            tok = ("d", eng, d)
            self.dtok[eng].append(tok)
            if d >= NDSEM:
                deps.add(("d", eng, d - NDSEM))
        else:
            tok = ("c", eng, idx)
        cmax = {}
        for t in deps:
            if t[0] == "c":
                _, e2, i2 = t
                if e2 == eng and eng == "pe":
                    continue
                if i2 <= self.seen_c[eng][e2]:
                    continue
                if i2 > cmax.get(e2, -1):
                    cmax[e2] = i2
            else:
                if t in self.seen_d[eng]:
                    continue
                self.seen_d[eng].add(t)
                waits.append(t)
        for e2, i2 in cmax.items():
            self.seen_c[eng][e2] = i2
            waits.append(("c", e2, i2))
        ops.append({"fn": fn, "waits": waits, "dma": dma, "sig": False, "tok": tok})
        for b in r:
            b.rd.append(tok)
            if len(b.rd) > 24:
                b.rd = self._compact(b.rd)
        for b in w:
            b.lw = tok
            b.rd = []
        return tok

    @staticmethod
    def _compact(rd):
        best = {}
        out = []
        for t in rd:
            if t[0] == "c":
                if t[2] > best.get(t[1], -1):
                    best[t[1]] = t[2]
            else:
                out.append(t)
        return out + [("c", e, i) for e, i in best.items()]

    @staticmethod
    def handoff(srcs, dsts):
        toks = []
        for b in srcs:
            if b.lw is not None:
                toks.append(b.lw)
            toks.extend(b.rd)
        for d in dsts:
            d.rd = list(d.rd) + toks

    def finalize(self):
        nc = self.nc
        for e in ENGS:
            for o in self.ops[e]:
                for t in o["waits"]:
                    if t[0] == "c":
                        self.ops[t[1]][t[2]]["sig"] = True
        csem = {}
        for e in ENGS:
            n = 0
            for o in self.ops[e]:
                if o["sig"] and not o["dma"]:
                    ep, v = divmod(n, EPOCH)
                    o["sv"] = (ep, v + 1)
                    n += 1
            nep = (n + EPOCH - 1) // EPOCH
            csem[e] = [self.stack.enter_context(nc.semaphore(f"c_{e}_{k}")) for k in range(nep)]
        dsem = {}
        for e in ENGS:
            k = min(self.ndma[e], NDSEM)
            dsem[e] = [self.stack.enter_context(nc.semaphore(f"d_{e}_{j}")) for j in range(k)]

        def resolve(t):
            if t[0] == "c":
                o = self.ops[t[1]][t[2]]
                ep, v = o["sv"]
                return csem[t[1]][ep], v
            _, e2, d = t
            return dsem[e2][d % NDSEM], 16 * (d // NDSEM + 1)

        def run(ename, eng):
            for o in self.ops[ename]:
                for t in o["waits"]:
                    s, v = resolve(t)
                    eng.wait_ge(s, v)
                ins = o["fn"](eng)
                if o["dma"]:
                    s, v = resolve(o["tok"])
                    ins.then_inc(s, 16)
                elif o["sig"]:
                    ep, v = o["sv"]
                    ins.then_inc(csem[ename][ep], 1)
            for t in self.dtok[ename][-NDSEM:]:
                s, v = resolve(t)
                eng.wait_ge(s, v)

        with nc.Block() as block:
            @block.tensor
            def _(e):
                run("pe", e)

            @block.scalar
            def _(e):
                run("act", e)

            @block.vector
            def _(e):
                run("dve", e)

            @block.gpsimd
            def _(e):
                run("pool", e)

            @block.sync
            def _(e):
                run("sp", e)

    def mm(self, out, out_ap, lhsT, lhsT_ap, rhs, rhs_ap, start=True, stop=True):
        return self.op("pe", lambda e: e.matmul(out_ap, lhsT_ap, rhs_ap, start=start, stop=stop),
                       r=(lhsT, rhs), w=(out,))

    def tr(self, out, out_ap, in_, in_ap, ident, ident_ap):
        return self.op("pe", lambda e: e.transpose(out_ap, in_ap, ident_ap), r=(in_, ident), w=(out,))

    def dma(self, eng, out, out_ap, in_, in_ap, **kw):
        r = (in_,) if in_ is not None else ()
        w = (out,) if out is not None else ()
        return self.op(eng, lambda e: e.dma_start(out=out_ap, in_=in_ap, **kw), r=r, w=w, dma=True)


def host_consts():
    j = np.arange(128)
    bf = ml_dtypes.bfloat16
    c = {}
    c["c_ident"] = np.eye(128).astype(bf)
    c["c_identf"] = np.eye(128, dtype=np.float32)
    c["c_negU"] = (-(j[:, None] >= j[None, :]).astype(np.float32)).astype(bf)
    c["c_negones"] = (-np.ones((128, 128), np.float32)).astype(bf)
    c["c_utrif"] = (j[:, None] <= j[None, :]).astype(np.float32)
    c["c_onesf"] = np.ones((128, 128), np.float32)
    c["c_mstrict"] = np.where(j[:, None] < j[None, :], 0.0, NEGM).astype(bf)
    c["c_mincl"] = np.where(j[:, None] <= j[None, :], 0.0, NEGM).astype(bf)
    sel = np.zeros((70, 6, 128), np.float32)
    for h in range(6):
        sel[h, h, :] = 1
        sel[32 + h, h, :] = 1
        sel[64 + h, h, :] = 1
    c["c_sel"] = sel.astype(bf)
    c["c_tril"] = (j[None, :] <= j[:, None]).astype(np.float32)
    same = (j[:, None] // 8) == (j[None, :] // 8)
    c["c_mblk_s"] = np.where(same & (j[:, None] % 8 < j[None, :] % 8), 0.0, NEGM).astype(bf)
    c["c_mblk_i"] = np.where(same & (j[:, None] % 8 <= j[None, :] % 8), 0.0, NEGM).astype(bf)
    c["c_bcum"] = (same & (j[:, None] <= j[None, :])).astype(np.float32)
    c["c_ugt"] = (j[:, None] > j[None, :]).astype(np.float32)
    r48 = np.arange(48)
    c["c_bmb"] = ((np.arange(384)[None, :] // 64) == (r48[:, None] // 8)).astype(bf)
    c["c_bmc"] = ((np.arange(390)[None, :] // 65) == (r48[:, None] // 8)).astype(bf)
    c["c_selw"] = (np.arange(248)[None, :] == (120 + r48[:, None] % 8)).astype(np.float32)
    return c


class K:
    def __init__(self, nc, st, with_sample):
        self.nc = nc
        self.P = Prog(nc, st)
        self.with_sample = with_sample
        self.cnt = 0
        P = self.P
        dt = lambda n, s, d, k="ExternalInput": nc.dram_tensor(n, list(s), d, kind=k).ap()
        self.xp = dt("xp", [S, D], F32)
        self.w_in = dt("w_in", [L, D, NIN], F32)
        self.w_o = dt("w_o", [L, D, D], F32)
        self.w_fi = dt("w_ffn_in", [L, D, 2 * DFF], F32)
        self.w_fo = dt("w_ffn_out", [L, DFF, D], F32)
        self.g_attn = dt("g_attn", [L, D], F32)
        self.g_mix = dt("g_mix", [L, D], F32)
        self.g_ffn = dt("g_ffn", [L, D], F32)
        self.g_final = dt("g_final", [D], F32)
        self.g_v = dt("g_v", [L, 256], F32)
        self.b_f = dt("b_f", [L, 6], F32)
        self.w_s = dt("w_s", [L, 4, 128, 128], F32)
        self.b_s = dt("b_s", [L, 4, 128], F32)
        self.cd = {}
        for n, a in host_consts().items():
            self.cd[n] = dt(n, a.shape, BF16 if a.dtype != np.float32 else F32)
        self.y_p = dt("y_p", [S, D], F32, "ExternalOutput")
        self.o_sbk = dt("o_sbk", [L, S, 384], F32, "ExternalOutput")
        self.o_sbv = dt("o_sbv", [L, S, 384], F32, "ExternalOutput")
        self.o_fk = dt("o_fk", [L, S, 384], F32, "ExternalOutput")
        self.o_fv = dt("o_fv", [L, S, 384], F32, "ExternalOutput")
        self.o_lf = dt("o_lf", [L, S, 6], F32, "ExternalOutput")
        self.xscr = dt("xscr", [S, D], F32, "ExternalOutput")
        self.xscr_b = [Buf(f"xscr{t}", None) for t in range(NT)]

        sb = P.sb
        self.ident = sb("ident", [128, 128], BF16)
        self.identf = sb("identf", [128, 128], F32)
        self.negU = sb("negU", [128, 128], BF16)
        self.negones = sb("negones", [128, 128], BF16)
        self.utrif = sb("utrif", [128, 128], F32)
        self.onesf = sb("onesf", [128, 128], F32)
        self.mstrict = sb("mstrict", [128, 128], BF16)
        self.mincl = sb("mincl", [128, 128], BF16)
        self.sel = sb("sel", [70, 6, 128], BF16)
        self.tril = sb("tril", [128, 128], F32)
        for b, n in ((self.ident, "c_ident"), (self.identf, "c_identf"), (self.negU, "c_negU"),
                     (self.negones, "c_negones"), (self.utrif, "c_utrif"), (self.onesf, "c_onesf"),
                     (self.mstrict, "c_mstrict"), (self.mincl, "c_mincl"), (self.sel, "c_sel"),
                     (self.tril, "c_tril")):
            P.dma("sp", b, b[:], None, self.cd[n])
        self.one = sb("one", [128, 1], F32)
        self.eps = sb("eps", [128, 1], F32)
        self.zt = sb("zt", [128, 512], BF16)
        P.op("dve", lambda e: e.memset(self.one[:], 1.0), w=(self.one,))
        P.op("dve", lambda e: e.memset(self.eps[:], 1e-6), w=(self.eps,))
        P.op("dve", lambda e: e.memset(self.zt[:], 0.0), w=(self.zt,))
        self.gcol = {k: sb("gcol_" + k, [128, 8], F32) for k in ("attn", "mix", "ffn")}
        self.gv_b = sb("gv_b", [128, 256], F32)
        self.bf_b = sb("bf_b", [128, 6], F32)
        self.bs_col = sb("bs_col", [128, 4], F32)
        self.bs_b = sb("bs_b", [128, 4, 64], F32)
        self.wsT = [sb(f"wsT{g}", [128, 128], BF16) for g in range(4)]
        self.wstmp = sb("wstmp", [128, 128], F32)
        self.psb = [P.ps(f"ps{i}", [128, 512], F32) for i in range(8)]
        self.psi = 0
        self.kbT = [[sb(f"kbT{p}_{g}", [128, 512 if g < NG else 128], BF16) for g in range(NG + 1)] for p in range(3)]
        self.kcT = [[sb(f"kcT{p}_{g}", [128, 512 if g < NG else 128], BF16) for g in range(NG + 1)] for p in range(3)]
        self.vb = [sb(f"vb{t}", [128, 384], BF16) for t in range(NT + 1)]
        self.vc = [sb(f"vc{t}", [128, 6, 65], BF16) for t in range(NT + 1)]
        for t in range(NT + 1):
            P.op("pool", lambda e, t=t: e.memset(self.vc[t][:], 1.0), w=(self.vc[t],))
        self.lfh = [sb(f"lfh{t}", [128, 6], F32) for t in range(NT + 1)]
        self.lfrep = [sb(f"lfrep{t}", [128, 96], F32) for t in range(NT + 1)]
        for t in range(NT + 1):
            P.op("pool", lambda e, t=t: e.memset(self.lfrep[t][:], 0.0), w=(self.lfrep[t],))
        self.negF = [sb(f"negF{t}", [128, 6], F32) for t in range(NT + 1)]
        self.xg = [sb(f"xg{i}", [128, D], F32) for i in range(GT)]
        self.xnT = sb("xnT", [128, 8, 512], BF16)
        self.qbT = [sb(f"qbT{p}", [128, 512], BF16) for p in range(3)]
        self.qcT = [sb(f"qcT{p}", [128, 512], BF16) for p in range(3)]
        self.Fp = sb("Fp", [70, 512], BF16)
        P.op("pool", lambda e: e.memset(self.Fp[:], 0.0), w=(self.Fp,))
        self.cat = [sb(f"cat{i}", [128, D], BF16) for i in range(GT)]
        self.otok = [sb(f"otok{i}", [128, 384], F32) for i in range(GT)]
        self.hT = sb("hT", [128, NJ, 512], BF16)
        def hview(k0, nk, dt_, name):
            ap = self.hT.t[:, k0:k0 + nk, :].rearrange("p a b -> p (a b)")
            return Buf(name, ap.bitcast(dt_) if dt_ != BF16 else ap)
        self.otokc = [hview(2 * i, 2, F32, f"otokc{i}") for i in range(GT)]
        self.gstag = [hview(k0, 7, F32, f"gstag{k0}") for k0 in (0, 8, 15)]
        self.OT = [hview(8 + 2 * i, 2, F32, f"OT{i}") for i in range(2)]
        self.hviews = self.otokc + self.gstag + self.OT
        self.wbuf = [sb(f"wbuf{i}", [128, 8, 512], BF16) for i in range(2)]
        self.wi = 0
        self.wh = [Buf(f"wh{i}_{k}", self.wbuf[i].t[:, :, 256 * k:256 * (k + 1)]) for i in range(2) for k in range(2)]
        self.wob = [sb(f"wob{i}", [128, D], BF16) for i in range(2)]
        self.woi = 0
        self.t512 = [sb(f"t512_{i}", [128, 512], F32) for i in range(3)]
        self.t512i = 0
        self.stg = [sb(f"stg{i}", [128, 392], F32) for i in range(4)]
        self.stgi = 0
        self.xs16 = [sb(f"xs16_{i}", [128, D], BF16) for i in range(2)]
        self.xs16i = 0
        self.sm = [sb(f"sm{i}", [128, 8], F32) for i in range(6)]
        self.smi = 0
        self.ebuf = [sb(f"ebuf{i}", [128, 512], F32) for i in range(2)]
        self.lbuf = [sb(f"lbuf{i}", [128, 512], BF16) for i in range(2)]
        self.abuf = [sb(f"abuf{i}", [128, 512], BF16) for i in range(2)]
        self.abufc = [sb(f"abufc{i}", [128, 512], BF16) for i in range(2)]
        self.acc = sb("acc", [128, 512], F32)
        self.accb = [sb(f"accb{i}", [128, 512], BF16) for i in range(2)]
        self.rot = {}
        self.lfo = sb("lfo", [128, GT, 6], F32)
        self.fpt = [sb(f"fpt{i}", [70, 128], F32) for i in range(2)]
        self.fpb = [sb(f"fpb{i}", [70, 128], BF16) for i in range(2)]
        self.gfin = None

    def nxt(self, key, lst):
        i = self.rot.get(key, 0)
        self.rot[key] = i + 1
        return lst[i % len(lst)]

    def ps(self, banks=(0, 1, 2, 3, 4, 5, 6, 7)):
        i = self.rot.get(("ps", banks), 0)
        self.rot[("ps", banks)] = i + 1
        return self.psb[banks[i % len(banks)]]

    def small(self):
        return self.nxt("sm", self.sm)

    def layer_params(self, l):
        P = self.P
        for k, src in (("attn", self.g_attn), ("mix", self.g_mix), ("ffn", self.g_ffn)):
            P.dma("sp", self.gcol[k], self.gcol[k][:], None, src[l].rearrange("(c p) -> p c", p=128),
                  allow_slow_non_contiguous=True)
        P.dma("sp", self.gv_b, self.gv_b[:], None, self.g_v[l].partition_broadcast(128))
        P.dma("sp", self.bf_b, self.bf_b[:], None, self.b_f[l].partition_broadcast(128))
        P.dma("sp", self.bs_col, self.bs_col[:], None, self.b_s[l].rearrange("g t -> t g"),
              allow_slow_non_contiguous=True)
        P.op("dve", lambda e: e.tensor_copy(out=self.bs_b[:], in_=self.bs_col[:, :].unsqueeze(2).to_broadcast([128, 4, 64])),
             r=(self.bs_col,), w=(self.bs_b,))
        for g in range(4):
            P.dma("sp", self.wstmp, self.wstmp[:], None, self.w_s[l, g])
            P.op("dve", lambda e: e.tensor_tensor(out=self.wstmp[:], in0=self.wstmp[:], in1=self.tril[:], op=ALU.mult),
                 r=(self.wstmp, self.tril), w=(self.wstmp,))
            ps = self.ps()
            P.tr(ps, ps[:, 0:128], self.wstmp, self.wstmp[:], self.identf, self.identf[:])
            P.op("dve", lambda e, g=g, ps=ps: e.tensor_copy(out=self.wsT[g][:], in_=ps[:, 0:128]), r=(ps,), w=(self.wsT[g],))

    def load_w(self, src_ap, ncols):
        wb = self.nxt("wbuf", self.wbuf)
        self.P.dma("pool", wb, wb[:, :, 0:ncols], None, src_ap.rearrange("(c p) n -> p c n", p=128))
        return wb

    def rms_stats(self, src_buf, src_ap, n):
        P = self.P
        sm = self.small()
        P.op("dve", lambda e: e.memset(sm[:], 0.0), w=(sm,))
        junk = self.nxt("t512", self.t512)
        w = src_ap.shape[-1]
        ncol = 0
        for c0 in range(0, w, 512):
            c1 = min(w, c0 + 512)
            P.op("act", lambda e, c0=c0, c1=c1, k=ncol: e.activation(out=junk[:, 0:c1 - c0], in_=src_ap[:, c0:c1], func=AF.Square,
                                                                      accum_out=sm[:, 1 + k:2 + k]),
                 r=(src_buf,), w=(junk, sm))
            ncol += 1
        if ncol > 1:
            P.op("dve", lambda e: e.tensor_tensor(out=sm[:, 1:2], in0=sm[:, 1:2], in1=sm[:, 2:3], op=ALU.add), r=(sm,), w=(sm,))
        P.op("act", lambda e: e.activation(out=sm[:, 0:1], in_=sm[:, 1:2], func=AF.Sqrt, scale=1.0 / n, bias=self.eps[:]),
             r=(sm, self.eps), w=(sm,))
        P.op("dve", lambda e: e.reciprocal(out=sm[:, 0:1], in_=sm[:, 0:1]), r=(sm,), w=(sm,))
        return sm

    def norm_to_T(self, xbuf, gkey, dstT, col0):
        P = self.P
        sm = self.rms_stats(xbuf, xbuf[:], D)
        xs = self.nxt("xs16", self.xs16)
        P.op("dve", lambda e: e.tensor_scalar(out=xs[:], in0=xbuf[:], scalar1=sm[:, 0:1], scalar2=None, op0=ALU.mult),
             r=(xbuf, sm), w=(xs,))
        self.to_T(xs, gkey, dstT, col0)

    def to_T(self, xs, gkey, dstT, col0):
        P = self.P
        ps = self.ps()
        pb = ps[:].bitcast(BF16)
        for c in range(8):
            P.tr(ps, pb[:, c * 128:(c + 1) * 128], xs, xs[:, c * 128:(c + 1) * 128], self.ident, self.ident[:])
        g = self.gcol[gkey]
        P.op("dve", lambda e: e.tensor_tensor(out=dstT[:, :, col0:col0 + 128], in0=pb.rearrange("p (c n) -> p c n", c=8),
                                             in1=g[:, :].unsqueeze(2).to_broadcast([128, 8, 128]), op=ALU.mult),
             r=(ps, g), w=(dstT,))

    def mm_tok(self, ps, ncols, xT, col0, wb, wc0):
        for c in range(8):
            self.P.mm(ps, ps[:, 0:ncols], xT, xT[:, c, col0:col0 + 128], wb, wb[:, c, wc0:wc0 + ncols],
                      start=(c == 0), stop=(c == 7))

    def mm_feat(self, ps, nrows, wb, wc0, xT, ntok):
        for c in range(8):
            self.P.mm(ps, ps[0:nrows, 0:ntok], wb, wb[:, c, wc0:wc0 + nrows], xT, xT[:, c, 0:ntok],
                      start=(c == 0), stop=(c == 7))

    def inproj(self, l, g, tiles, ntok, outs):
        P = self.P
        import os
        self.ipmax = int(os.environ.get("IPMAX", 99))
        nt = len(tiles)
        xT = self.xnT
        if self.ipmax <= 0:
            return
        wb = self.load_w(self.w_in[l][:, 0:512], 512)
        for i in range(nt):
            ps = self.ps()
            self.mm_tok(ps, 512, xT, 128 * i, wb, 0)
            self.gate(ps, i, outs)
        if self.ipmax <= 1:
            return
        wb = self.load_w(self.w_in[l][:, 512:896], 384)
        for p in range(3):
            ps = self.ps()
            self.mm_feat(ps, 128, wb, 128 * p, xT, ntok)
            P.op("act", lambda e, p=p, ps=ps: e.activation(out=self.qbT[p][:, 0:ntok], in_=ps[:, 0:ntok], func=AF.Copy, scale=0.125),
                 r=(ps,), w=(self.qbT[p],))
        if self.ipmax <= 2:
            return
        wb = self.load_w(self.w_in[l][:, 896:1280], 384)
        for p in range(3):
            ps = self.ps()
            self.mm_feat(ps, 128, wb, 128 * p, xT, ntok)
            dst = self.kT_dst("b", p, g)
            P.op("dve", lambda e, ps=ps, dst=dst: e.tensor_copy(out=dst[:, 0:ntok], in_=ps[:, 0:ntok]), r=(ps,), w=(dst,))
        for i in range(nt):
            ps = self.ps()
            self.mm_tok(ps, 384, xT, 128 * i, wb, 0)
            self.out_tok(ps, outs["sbk"], i, None)
        if self.ipmax <= 3:
            return
        wb = self.load_w(self.w_in[l][:, 1280:1664], 384)
        for i in range(nt):
            ps = self.ps()
            self.mm_tok(ps, 384, xT, 128 * i, wb, 0)
            self.out_tok(ps, outs["sbv"], i, ("vb", tiles[i]))
        if self.ipmax <= 4:
            return
        wb = self.load_w(self.w_in[l][:, 1664:2048], 384)
        for p in range(3):
            ps = self.ps()
            self.mm_feat(ps, 128, wb, 128 * p, xT, ntok)
            P.op("act", lambda e, p=p, ps=ps: e.activation(out=self.qcT[p][:, 0:ntok], in_=ps[:, 0:ntok], func=AF.Copy, scale=0.125),
                 r=(ps,), w=(self.qcT[p],))
        if self.ipmax <= 5:
            return
        wb = self.load_w(self.w_in[l][:, 2048:2432], 384)
        for p in range(3):
            ps = self.ps()
            self.mm_feat(ps, 128, wb, 128 * p, xT, ntok)
            dst = self.kT_dst("c", p, g)
            P.op("dve", lambda e, ps=ps, dst=dst: e.tensor_copy(out=dst[:, 0:ntok], in_=ps[:, 0:ntok]), r=(ps,), w=(dst,))
        for i in range(nt):
            ps = self.ps()
            self.mm_tok(ps, 384, xT, 128 * i, wb, 0)
            self.out_tok(ps, outs["fk"], i, None)
        if self.ipmax <= 6:
            return
        wb = self.load_w(self.w_in[l][:, 2432:2822], 390)
        for i in range(nt):
            ps = self.ps()
            self.mm_tok(ps, 390, xT, 128 * i, wb, 0)
            st = self.out_tok(ps, outs["fv"], i, ("vc", tiles[i]), ncols=390)
            self.logf(st, i, tiles[i], outs)

    def kT_dst(self, kind, p, g):
        return (self.kbT if kind == "b" else self.kcT)[p][g]

    def out_tok(self, ps, dram_ap, i, keep, ncols=384):
        P = self.P
        st = self.nxt("stg", self.stg)
        P.op("act", lambda e: e.activation(out=st[:, 0:ncols], in_=ps[:, 0:ncols], func=AF.Copy), r=(ps,), w=(st,))
        P.dma("sp", None, dram_ap[128 * i:128 * (i + 1), :], st, st[:, 0:384])
        if keep is not None:
            kind, t = keep
            if kind == "vb":
                dst = self.vb_dst(t)
                P.op("dve", lambda e: e.tensor_copy(out=dst[:], in_=st[:, 0:384]), r=(st,), w=(dst,))
            else:
                dst = self.vc_dst(t)
                P.op("dve", lambda e: e.tensor_copy(out=dst[:, :, 0:64], in_=st[:, 0:384].rearrange("p (h d) -> p h d", h=6)),
                     r=(st,), w=(dst,))
        return st

    def vb_dst(self, t):
        return self.vb[t]

    def vc_dst(self, t):
        return self.vc[t]

    def gate(self, ps, i, outs):
        P = self.P
        import os
        if os.environ.get("NOGATE"):
            return
        uv = self.nxt("t512", self.t512)
        P.op("act", lambda e: e.activation(out=uv[:], in_=ps[:], func=AF.Gelu_apprx_tanh), r=(ps,), w=(uv,))
        sm = self.rms_stats(uv, uv[:, 256:512], 256)
        vn32 = self.nxt("stg", self.stg)
        P.op("dve", lambda e: e.scalar_tensor_tensor(out=vn32[:, 0:256], in0=uv[:, 256:512], scalar=sm[:, 0:1], in1=self.gv_b[:],
                                                     op0=ALU.mult, op1=ALU.mult), r=(uv, sm, self.gv_b), w=(vn32,))
        if outs.get("cv") is not None:
            P.dma("sp", None, outs["cv"][128 * i:128 * (i + 1), :], vn32, vn32[:, 0:256])
        vn = self.nxt("xs16", self.xs16)
        P.op("dve", lambda e: e.tensor_copy(out=vn[:, 0:256], in_=vn32[:, 0:256]), r=(vn32,), w=(vn,))
        pm = self.ps()
        for g in range(4):
            P.mm(pm, pm[:, 64 * g:64 * (g + 1)], self.wsT_cur[g], self.wsT_cur[g][:], vn, vn[:, 64 * g:64 * (g + 1)])
        a = self.nxt("t512", self.t512)
        bs = self.bs_cur
        P.op("dve", lambda e: e.tensor_tensor(out=a[:, 0:256], in0=pm[:, 0:256], in1=bs[:].rearrange("p g d -> p (g d)"), op=ALU.add),
             r=(pm, bs), w=(a,))
        P.op("dve", lambda e: e.tensor_tensor(out=a[:, 0:256], in0=a[:, 0:256], in1=uv[:, 0:256], op=ALU.mult), r=(a, uv), w=(a,))
        self.norm_into_cat(a, a[:, 0:256], 256, i, 0)

    def norm_into_cat(self, buf, ap, n, i, c0):
        P = self.P
        sm = self.rms_stats(buf, ap, n)
        cat = self.cat[i]
        P.op("dve", lambda e: e.tensor_scalar(out=cat[:, c0:c0 + n], in0=ap, scalar1=sm[:, 0:1], scalar2=None, op0=ALU.mult),
             r=(buf, sm), w=(cat,))

    def logf(self, ps, i, t, outs):
        P = self.P
        import os
        if os.environ.get("NOLOGF"):
            return
        sm = self.small()
        P.op("dve", lambda e: e.tensor_tensor(out=sm[:, 0:6], in0=ps[:, 384:390], in1=self.bf_b[:], op=ALU.add), r=(ps, self.bf_b), w=(sm,))
        P.op("act", lambda e: e.activation(out=sm[:, 0:6], in_=sm[:, 0:6], func=AF.Exp, scale=-1.0), r=(sm,), w=(sm,))
        P.op("act", lambda e: e.activation(out=sm[:, 0:6], in_=sm[:, 0:6], func=AF.Ln, bias=self.one[:], scale=1.0), r=(sm, self.one), w=(sm,))
        lf = self.lfh[t]
        P.op("dve", lambda e: e.tensor_scalar(out=lf[:], in0=sm[:, 0:6], scalar1=-1.0, scalar2=None, op0=ALU.mult), r=(sm,), w=(lf,))
        P.op("pool", lambda e: e.tensor_copy(out=self.lfo[:, i, :], in_=lf[:]), r=(lf,), w=(self.lfo,))
        if not self.prompt_mode:
            ps2 = self.ps()
            P.mm(ps2, ps2[:, 0:6], self.bcum, self.bcum[:], lf, lf[:])
            P.op("dve", lambda e: e.tensor_scalar(out=self.negF[t][:], in0=ps2[:, 0:6], scalar1=-1.0, scalar2=None, op0=ALU.mult),
                 r=(ps2,), w=(self.negF[t],))
        if self.prompt_mode:
            lr = self.lfrep[t]
            P.op("dve", lambda e: e.tensor_copy(out=lr[:, 0:96].rearrange("p (a b) -> p a b", a=3)[:, :, 0:6],
                                                in_=lf[:, :].unsqueeze(1).to_broadcast([128, 3, 6])), r=(lf,), w=(lr,))
            ps2 = self.ps()
            for t2 in range(t + 1):
                m = self.utrif if t2 == t else self.onesf
                P.mm(ps2, ps2[:, 0:6], m, m[:], self.lfh[t2], self.lfh[t2][:], start=(t2 == 0), stop=(t2 == t))
            P.op("dve", lambda e: e.tensor_scalar(out=self.negF[t][:], in0=ps2[:, 0:6], scalar1=-1.0, scalar2=None, op0=ALU.mult),
                 r=(ps2,), w=(self.negF[t],))
            ps3 = self.ps()
            for t2 in range(t + 1):
                m = self.utrif if t2 == t else self.onesf
                P.mm(ps3, ps3[0:70, 0:128], self.lfrep[t2], self.lfrep[t2][:, 0:70], m, m[:], start=(t2 == 0), stop=(t2 == t))
            hiA = self.nxt("fpb", self.fpb)
            r1 = self.nxt("fpt", self.fpt)
            midA = self.nxt("fpb", self.fpb)
            r2 = self.nxt("fpt", self.fpt)
            P.op("dve", lambda e: e.tensor_copy(out=hiA[:], in_=ps3[0:70, 0:128]), r=(ps3,), w=(hiA,))
            P.op("dve", lambda e: e.tensor_tensor(out=r1[:], in0=ps3[0:70, 0:128], in1=hiA[:], op=ALU.subtract), r=(ps3, hiA), w=(r1,))
            P.op("dve", lambda e: e.tensor_copy(out=midA[:], in_=r1[:]), r=(r1,), w=(midA,))
            P.op("dve", lambda e: e.tensor_tensor(out=r2[:], in0=r1[:], in1=midA[:], op=ALU.subtract), r=(r1, midA), w=(r2,))
            c0 = 128 * i
            P.op("pool", lambda e: e.tensor_copy(out=self.Fp[0:6, c0:c0 + 128], in_=hiA[0:6, :]), r=(hiA,), w=(self.Fp,))
            P.op("pool", lambda e: e.tensor_copy(out=self.Fp[32:38, c0:c0 + 128], in_=midA[32:38, :]), r=(midA,), w=(self.Fp,))
            P.op("pool", lambda e: e.tensor_copy(out=self.Fp[64:70, c0:c0 + 128], in_=r2[64:70, :]), r=(r2,), w=(self.Fp,))

    def attn_begin(self, kind, N, psO):
        nrow = 64 if kind == "b" else 65
        self.P.mm(psO, psO[0:nrow, 0:N], self.zt, self.zt[:, 0:nrow], self.zt, self.zt[:, 0:N], start=True, stop=False)

    def attn_s1(self, kind, h, qT, q0, N, unit):
        P = self.P
        p, r0 = h // 2, 64 * (h % 2)
        kb, qlo, mask, use_sel = unit
        kT = self.kT_dst(kind, p, kb // GT)
        kc0 = 128 * (kb % GT)

        def zmm(ps, final_stop):
            mms = [(ps[:, qlo:N], kT, kT[r0:r0 + 64, kc0:kc0 + 128], qT, qT[r0:r0 + 64, q0 + qlo:q0 + N])]
            if kind == "c" and use_sel:
                mms.append((ps[:, qlo:N], self.sel, self.sel[:, h, :], self.Fp, self.Fp[:, qlo:N]))
            if mask is not None:
                mb, map_, mw = mask
                mms.append((ps[:, qlo:qlo + mw], self.ident, self.ident[:], mb, map_))
            for n_, (o_ap, lb, l_ap, rb, r_ap) in enumerate(mms):
                P.mm(ps, o_ap, lb, l_ap, rb, r_ap, start=(n_ == 0), stop=(final_stop and n_ == len(mms) - 1))

        c = {"kind": kind, "h": h, "kb": kb, "qlo": qlo, "N": N, "zmm": zmm}
        if kind == "b":
            psZ = self.ps((0, 1))
            zmm(psZ, True)
            e_ = self.nxt("ebuf", self.ebuf)
            l_ = self.nxt("lbuf", self.lbuf)
            P.op("act", lambda e: e.activation(out=e_[:, qlo:N], in_=psZ[:, qlo:N], func=AF.Exp), r=(psZ,), w=(e_,))
            P.op("act", lambda e: e.activation(out=l_[:, qlo:N], in_=e_[:, qlo:N], func=AF.Ln, bias=self.one[:], scale=1.0),
                 r=(e_, self.one), w=(l_,))
            c["l"] = l_
        else:
            psZ = self.ps((6, 7))
            zmm(psZ, True)
            a_ = self.nxt("abufc", self.abufc)
            nf = self.negF[kb]
            P.op("act", lambda e: e.activation(out=a_[:, qlo:N], in_=psZ[:, qlo:N], func=AF.Exp, bias=nf[:, h:h + 1], scale=1.0),
                 r=(psZ, nf), w=(a_,))
            c["a"] = a_
        return c

    def attn_s2(self, c, psO, first, last, st):
        P = self.P
        kind, h, kb, qlo, N = c["kind"], c["h"], c["kb"], c["qlo"], c["N"]
        if kind == "b":
            l_ = c["l"]
            psE = self.ps((2, 3))
            c["zmm"](psE, False)
            P.mm(psE, psE[:, qlo:N], self.negU, self.negU[:], l_, l_[:, qlo:N], start=False, stop=first)
            if not first:
                ab_ = st["accb"]
                P.mm(psE, psE[:, qlo:N], self.negones, self.negones[:], ab_, ab_[:, qlo:N], start=False, stop=True)
            a_ = self.nxt("abuf", self.abuf)
            P.op("act", lambda e: e.activation(out=a_[:, qlo:N], in_=psE[:, qlo:N], func=AF.Exp), r=(psE,), w=(a_,))
            if not last:
                if first:
                    P.op("dve", lambda e: e.memset(self.acc[:, 0:N], 0.0), w=(self.acc,))
                P.op("dve", lambda e: e.tensor_tensor(out=self.acc[:, qlo:N], in0=self.acc[:, qlo:N], in1=l_[:, qlo:N], op=ALU.add),
                     r=(self.acc, l_), w=(self.acc,))
                nb = self.nxt("accb", self.accb)
                P.op("dve", lambda e: e.tensor_copy(out=nb[:, 0:N], in_=self.acc[:, 0:N]), r=(self.acc,), w=(nb,))
                st["accb"] = nb
            vt = self.vb[kb]
            P.mm(psO, psO[0:64, qlo:N], vt, vt[:, 64 * h:64 * h + 64], a_, a_[:, qlo:N], start=False, stop=last)
        else:
            a_ = c["a"]
            vt = self.vc[kb]
            P.mm(psO, psO[0:65, qlo:N], vt, vt[:, h, :], a_, a_[:, qlo:N], start=False, stop=last)

    def o_to_tok(self, kind, h, ot_buf, ot_ap, ntile):
        P = self.P
        nrow = 64 if kind == "b" else 65
        pt = self.ps((6, 7))
        for i in range(ntile):
            P.tr(pt, pt[:, 128 * i:128 * i + nrow], ot_buf, ot_ap[0:nrow, 128 * i:128 * (i + 1)], self.identf, self.identf[0:nrow, 0:nrow])
        for i in range(ntile):
            o = self.otok[i] if kind == "b" else self.otokc[i]
            if kind == "b":
                P.op("dve", lambda e, o=o, pt=pt, i=i, h=h: e.tensor_copy(out=o[:, 64 * h:64 * h + 64], in_=pt[:, 128 * i:128 * i + 64]),
                     r=(pt,), w=(o,))
            else:
                rc = self.small()
                P.op("dve", lambda e, rc=rc, pt=pt, i=i: e.reciprocal(out=rc[:, 0:1], in_=pt[:, 128 * i + 64:128 * i + 65]), r=(pt,), w=(rc,))
                P.op("dve", lambda e, o=o, pt=pt, i=i, h=h, rc=rc: e.tensor_scalar(out=o[:, 64 * h:64 * h + 64], in0=pt[:, 128 * i:128 * i + 64],
                                                                               scalar1=rc[:, 0:1], scalar2=None, op0=ALU.mult),
                     r=(pt, rc), w=(o,))

    def attn_prompt(self, g):
        P = self.P
        for h in range(6):
            psO = {"b": self.psb[4], "c": self.psb[5]}
            qT = {"b": self.qbT[h // 2], "c": self.qcT[h // 2]}
            units = {"b": [], "c": []}
            for kind in ("b", "c"):
                for kb in range(GT * g + GT - 1, -1, -1):
                    qlo = max(0, kb - GT * g) * 128
                    m = None
                    if kb >= GT * g:
                        mb = self.mstrict if kind == "b" else self.mincl
                        m = (mb, mb[:], 128)
                    units[kind].append((kb, qlo, m, True))
                self.attn_begin(kind, 512, psO[kind])
            st = {}
            nu = len(units["b"])
            cur = {kind: self.attn_s1(kind, h, qT[kind], 0, 512, units[kind][0]) for kind in ("b", "c")}
            for k in range(nu):
                nxt_ = {}
                if k + 1 < nu:
                    nxt_ = {kind: self.attn_s1(kind, h, qT[kind], 0, 512, units[kind][k + 1]) for kind in ("b", "c")}
                for kind in ("b", "c"):
                    self.attn_s2(cur[kind], psO[kind], k == 0, k == nu - 1, st)
                cur = nxt_
            for kind in ("b", "c"):
                nrow = 64 if kind == "b" else 65
                ot = self.nxt("OT", self.OT)
                po = psO[kind]
                P.op("dve", lambda e, po=po, ot=ot, nrow=nrow: e.tensor_copy(out=ot[0:nrow, :], in_=po[0:nrow, :]), r=(po,), w=(ot,))
                self.o_to_tok(kind, h, ot, ot[:], GT)
        for kind in ("b", "c"):
            for i in range(GT):
                ok = self.otok[i] if kind == "b" else self.otokc[i]
                self.norm_into_cat(ok, ok[:, 0:384], 384, i, 256 if kind == "b" else 640)

    def sample_setup(self):
        P = self.P
        nc = self.nc
        dt = lambda n, s_, d, k="ExternalInput": nc.dram_tensor(n, list(s_), d, kind=k).ap()
        self.xs_d = dt("xs", [128, D], F32)
        self.pt_d = dt("pt", [1, 256], I32)
        R = L * NPOOL * 128
        self.c_all = dt("c_all", [R, 1542], F32)
        self.y_s = dt("y_s", [128, D], F32, "ExternalOutput")
        self.s_sbk = dt("s_sbk", [L, 128, 384], F32, "ExternalOutput")
        self.s_sbv = dt("s_sbv", [L, 128, 384], F32, "ExternalOutput")
        self.s_fk = dt("s_fk", [L, 128, 384], F32, "ExternalOutput")
        self.s_fv = dt("s_fv", [L, 128, 384], F32, "ExternalOutput")
        self.s_lf = dt("s_lf", [L, 128, 6], F32, "ExternalOutput")
        self.s_cv = dt("s_cv", [L, 128, 256], F32, "ExternalOutput")
        sb = P.sb
        self.xsr = sb("xsr", [128, D], F32)
        P.dma("sp", self.xsr, self.xsr[:], None, self.xs_d)
        self.mblk_s = sb("mblk_s", [128, 128], BF16)
        self.mblk_i = sb("mblk_i", [128, 128], BF16)
        self.bcum = sb("bcum", [128, 128], F32)
        P.dma("sp", self.mblk_s, self.mblk_s[:], None, self.cd["c_mblk_s"])
        P.dma("sp", self.mblk_i, self.mblk_i[:], None, self.cd["c_mblk_i"])
        P.dma("sp", self.bcum, self.bcum[:], None, self.cd["c_bcum"])
        ptb = sb("ptb", [128, 256], I32)
        iot = sb("iot", [128, 1], I32)
        iotf = sb("iotf", [128, 1], F32)
        P.dma("sp", ptb, ptb[:], None, self.pt_d[0, :].partition_broadcast(128))
        P.op("pool", lambda e: e.iota(iot[:], pattern=[[0, 1]], base=0, channel_multiplier=1), w=(iot,))
        P.op("dve", lambda e: e.tensor_copy(out=iotf[:], in_=iot[:]), r=(iot,), w=(iotf,))
        self.idx = []
        for l in range(L):
            ix = sb(f"idx{l}", [128, 256], I32)
            io2 = sb(f"iotf{l}", [128, 1], F32)
            P.op("dve", lambda e, io2=io2, l=l: e.tensor_scalar(out=io2[:], in0=iotf[:], scalar1=float(l * NPOOL * 128), scalar2=None, op0=ALU.add),
                 r=(iotf,), w=(io2,))
            P.op("dve", lambda e, ix=ix, io2=io2: e.tensor_scalar(out=ix[:], in0=ptb[:], scalar1=128.0, scalar2=io2[:, 0:1], op0=ALU.mult, op1=ALU.add),
                 r=(ptb, io2), w=(ix,))
            self.idx.append(ix)
        self.QB = [sb(f"QB{p}", [128, 256], BF16) for p in range(3)]
        self.QC = [sb(f"QC{p}", [128, 256], BF16) for p in range(3)]
        for q in self.QB + self.QC:
            P.op("pool", lambda e, q=q: e.memset(q[:], 0.0), w=(q,))
        self.m6 = {"b": sb("m6b", [128, 48], BF16), "c": sb("m6c", [128, 48], BF16)}
        self.gc = sb("gc", [128, 6], F32)
        self.bmb = sb("bmb", [48, 384], BF16)
        self.bmc = sb("bmc", [48, 390], BF16)
        self.selw = sb("selw", [48, 248], F32)
        P.dma("sp", self.bmb, self.bmb[:], None, self.cd["c_bmb"])
        P.dma("sp", self.bmc, self.bmc[:], None, self.cd["c_bmc"])
        P.dma("sp", self.selw, self.selw[:], None, self.cd["c_selw"])
        self.ugt = sb("ugt", [128, 128], F32)
        P.dma("sp", self.ugt, self.ugt[:], None, self.cd["c_ugt"])
        self.wsTs = [sb(f"wsTs{g}", [128, 128], BF16) for g in range(4)]
        self.bs_s = sb("bs_s", [128, 4, 64], F32)
        self.bs_col_s = sb("bs_col_s", [128, 4], F32)

    def sample_params(self, l):
        P = self.P
        for i in range(16):
            P.dma("sp", self.bs_col_s, self.bs_col_s[8 * i:8 * i + 8, :], None, self.b_s[l][:, 0:8].rearrange("g t -> t g"),
                  allow_slow_non_contiguous=True)
        P.op("dve", lambda e: e.tensor_copy(out=self.bs_s[:], in_=self.bs_col_s[:, :].unsqueeze(2).to_broadcast([128, 4, 64])),
             r=(self.bs_col_s,), w=(self.bs_s,))
        for g in range(4):
            P.op("pool", lambda e: e.memset(self.wstmp[:], 0.0), w=(self.wstmp,))
            for i in range(16):
                P.dma("sp", self.wstmp, self.wstmp[8 * i:8 * i + 8, 8 * i:8 * i + 8], None, self.w_s[l, g, 0:8, 0:8])
            P.op("dve", lambda e: e.tensor_tensor(out=self.wstmp[:], in0=self.wstmp[:], in1=self.tril[:], op=ALU.mult),
                 r=(self.wstmp, self.tril), w=(self.wstmp,))
            ps = self.ps()
            P.tr(ps, ps[:, 0:128], self.wstmp, self.wstmp[:], self.identf, self.identf[:])
            P.op("dve", lambda e, g=g, ps=ps: e.tensor_copy(out=self.wsTs[g][:], in_=ps[:, 0:128]), r=(ps,), w=(self.wsTs[g],))

    def gather_page(self, l, i, j):
        P = self.P
        ix = self.idx[l]
        off = bass.IndirectOffsetOnAxis(ap=ix[:, 16 * i + j:16 * i + j + 1], axis=0)
        st = self.gstag[self.gn % len(self.gstag)]
        self.gn += 1
        P.op("pool", lambda e: e.indirect_dma_start(out=st[:, 0:1542], out_offset=None, in_=self.c_all, in_offset=off),
             r=(ix,), w=(st,), dma=True)
        vb_, vc_ = self.vb[j], self.vc[j]
        P.op("act", lambda e: e.activation(out=vb_[:], in_=st[:, 384:768], func=AF.Copy), r=(st,), w=(vb_,))
        P.op("act", lambda e: e.activation(out=vc_[:, :, 0:64], in_=st[:, 1152:1536].rearrange("p (h d) -> p h d", h=6), func=AF.Copy),
             r=(st,), w=(vc_,))
        for kind, c0 in (("b", 0), ("c", 768)):
            kb16 = self.nxt("xs16", self.xs16)
            P.op("dve", lambda e, kb16=kb16, c0=c0: e.tensor_copy(out=kb16[:, 0:384], in_=st[:, c0:c0 + 384]), r=(st,), w=(kb16,))
            ps = self.ps((6, 7))
            pb = ps[:].bitcast(BF16)
            for p in range(3):
                P.tr(ps, pb[:, 128 * p:128 * (p + 1)], kb16, kb16[:, 128 * p:128 * (p + 1)], self.ident, self.ident[:])
            for p in range(3):
                dst = self.kT_dst(kind, p, j // GT)
                cc = 128 * (j % GT)
                P.op("dve", lambda e, dst=dst, pb=pb, p=p, cc=cc: e.tensor_copy(out=dst[:, cc:cc + 128], in_=pb[:, 128 * p:128 * (p + 1)]),
                     r=(ps,), w=(dst,))
        ps2 = self.ps((6, 7))
        P.mm(ps2, ps2[:, 0:6], self.ugt, self.ugt[:], st, st[:, 1536:1542])
        P.mm(ps2, ps2[:, 8:14], self.onesf, self.onesf[:], st, st[:, 1536:1542])
        G = self.negF[j]
        if j == 15:
            P.op("dve", lambda e: e.tensor_copy(out=G[:], in_=ps2[:, 0:6]), r=(ps2,), w=(G,))
            P.op("dve", lambda e: e.tensor_copy(out=self.gc[:], in_=ps2[:, 8:14]), r=(ps2,), w=(self.gc,))
        else:
            P.op("dve", lambda e: e.tensor_tensor(out=G[:], in0=ps2[:, 0:6], in1=self.gc[:], op=ALU.add), r=(ps2, self.gc), w=(G,))
            P.op("dve", lambda e: e.tensor_tensor(out=self.gc[:], in0=ps2[:, 8:14], in1=self.gc[:], op=ALU.add), r=(ps2, self.gc), w=(self.gc,))

    def s_s1(self, kind, i, kb, mask6):
        P = self.P
        Q = self.QB if kind == "b" else self.QC

        def zmm(ps, final_stop):
            mms = []
            for p in range(3):
                kT = self.kT_dst(kind, p, kb // GT)
                kc0 = 128 * (kb % GT)
                mms.append((ps[:, 16 * p:16 * p + 16], kT, kT[:, kc0:kc0 + 128], Q[p], Q[p][:, 16 * i:16 * i + 16]))
            if mask6 is not None:
                mms.append((ps[:, 0:48], self.ident, self.ident[:], mask6, mask6[:]))
            for n_, (o_ap, lb, l_ap, rb, r_ap) in enumerate(mms):
                P.mm(ps, o_ap, lb, l_ap, rb, r_ap, start=(n_ == 0), stop=(final_stop and n_ == len(mms) - 1))

        c = {"kind": kind, "kb": kb, "zmm": zmm}
        if kind == "b":
            psZ = self.ps((0, 1))
            zmm(psZ, True)
            e_ = self.nxt("ebuf", self.ebuf)
            l_ = self.nxt("lbuf", self.lbuf)
            P.op("act", lambda e: e.activation(out=e_[:, 0:48], in_=psZ[:, 0:48], func=AF.Exp), r=(psZ,), w=(e_,))
            P.op("act", lambda e: e.activation(out=l_[:, 0:48], in_=e_[:, 0:48], func=AF.Ln, bias=self.one[:], scale=1.0),
                 r=(e_, self.one), w=(l_,))
            c["l"] = l_
        else:
            psZ = self.ps((6, 7))
            zmm(psZ, True)
            tmp = self.nxt("t512", self.t512)
            G = self.negF[kb]
            P.op("dve", lambda e: e.tensor_tensor(out=tmp[:, 0:48].rearrange("p (h t) -> p h t", h=6), in0=psZ[:, 0:48].rearrange("p (h t) -> p h t", h=6),
                                                 in1=G[:, :].unsqueeze(2).to_broadcast([128, 6, 8]), op=ALU.add), r=(psZ, G), w=(tmp,))
            a_ = self.nxt("abufc", self.abufc)
            P.op("act", lambda e: e.activation(out=a_[:, 0:48], in_=tmp[:, 0:48], func=AF.Exp), r=(tmp,), w=(a_,))
            c["a"] = a_
        return c

    def s_s2(self, c, psO, first, last):
        P = self.P
        kind, kb = c["kind"], c["kb"]
        if kind == "b":
            l_ = c["l"]
            psE = self.ps((2, 3))
            c["zmm"](psE, False)
            P.mm(psE, psE[:, 0:48], self.negU, self.negU[:], l_, l_[:, 0:48], start=False, stop=first)
            if not first:
                ab_ = self.accb_cur
                P.mm(psE, psE[:, 0:48], self.negones, self.negones[:], ab_, ab_[:, 0:48], start=False, stop=True)
            a_ = self.nxt("abuf", self.abuf)
            P.op("act", lambda e: e.activation(out=a_[:, 0:48], in_=psE[:, 0:48], func=AF.Exp), r=(psE,), w=(a_,))
            if not last:
                if first:
                    P.op("dve", lambda e: e.tensor_copy(out=self.acc[:, 0:48], in_=l_[:, 0:48]), r=(l_,), w=(self.acc,))
                else:
                    P.op("dve", lambda e: e.tensor_tensor(out=self.acc[:, 0:48], in0=self.acc[:, 0:48], in1=l_[:, 0:48], op=ALU.add),
                         r=(self.acc, l_), w=(self.acc,))
                nb = self.nxt("accb", self.accb)
                P.op("dve", lambda e: e.tensor_copy(out=nb[:, 0:48], in_=self.acc[:, 0:48]), r=(self.acc,), w=(nb,))
                self.accb_cur = nb
            vt = self.vb[kb]
            P.mm(psO, psO[0:48, 0:384], a_, a_[:, 0:48], vt, vt[:], start=first, stop=last)
        else:
            a_ = c["a"]
            vt = self.vc[kb]
            P.mm(psO, psO[0:48, 0:390], a_, a_[:, 0:48], vt, vt[:].rearrange("p h d -> p (h d)"), start=first, stop=last)

    def attn_sample(self, l):
        P = self.P
        self.gn = 0
        for kind, Q, qT in (("b", self.QB, self.qbT), ("c", self.QC, self.qcT)):
            for p in range(3):
                for half in range(2):
                    r0 = 64 * half
                    P.op("dve", lambda e, Q=Q, qT=qT, p=p, r0=r0, half=half: e.tensor_copy(
                        out=Q[p][r0:r0 + 64, :].rearrange("p (i c) -> p i c", c=16)[:, :, 8 * half:8 * half + 8],
                        in_=qT[p][r0:r0 + 64, 0:128].rearrange("p (i t) -> p i t", t=8)), r=(qT[p],), w=(Q[p],))
        for i in range(16):
            m6 = {}
            for kind, mb in (("b", self.mblk_s), ("c", self.mblk_i)):
                m = self.m6[kind]
                P.op("dve", lambda e, m=m, mb=mb, i=i: e.tensor_copy(out=m[:, :].rearrange("p (h t) -> p h t", h=6),
                                                                   in_=mb[:, 8 * i:8 * i + 8].unsqueeze(1).to_broadcast([128, 6, 8])), r=(mb,), w=(m,))
                m6[kind] = m
            psOb = self.psb[4]
            psOc = self.psb[5]
            cur = [self.s_s1("b", i, NT, m6["b"]), self.s_s1("c", i, NT, m6["c"])]
            first = True
            for j in range(15, -2, -1):
                nxt_ = None
                if j >= 0:
                    self.gather_page(l, i, j)
                    nxt_ = [self.s_s1("b", i, j, None), self.s_s1("c", i, j, None)]
                self.s_s2(cur[0], psOb, first, j < 0)
                self.s_s2(cur[1], psOc, first, j < 0)
                first = False
                cur = nxt_
            accb_, accc_ = self.otok[0], self.xg[1]
            for kind, po, bm, ncol, acc_ in (("b", psOb, self.bmb, 384, accb_), ("c", psOc, self.bmc, 390, accc_)):
                mk = self.xg[2]
                P.op("dve", lambda e, po=po, bm=bm, ncol=ncol, mk=mk: e.tensor_tensor(out=mk[0:48, 0:ncol], in0=po[0:48, 0:ncol], in1=bm[:, 0:ncol], op=ALU.mult),
                     r=(po, bm), w=(mk,))
                pt = self.ps((6, 7))
                P.mm(pt, pt[:, 0:ncol], self.selw, self.selw[:, 120 - 8 * i:248 - 8 * i], mk, mk[0:48, 0:ncol])
                if i == 0:
                    P.op("dve", lambda e, pt=pt, acc_=acc_, ncol=ncol: e.tensor_copy(out=acc_[:, 0:ncol], in_=pt[:, 0:ncol]), r=(pt,), w=(acc_,))
                else:
                    P.op("dve", lambda e, pt=pt, acc_=acc_, ncol=ncol: e.tensor_tensor(out=acc_[:, 0:ncol], in0=acc_[:, 0:ncol], in1=pt[:, 0:ncol], op=ALU.add),
                         r=(pt, acc_), w=(acc_,))
        P.handoff(self.gstag, self.otokc)
        accc_ = self.xg[1]
        rc = self.small()
        P.op("dve", lambda e: e.reciprocal(out=rc[:, 0:6], in_=accc_[:, 0:390].rearrange("p (h d) -> p h d", h=6)[:, :, 64]), r=(accc_,), w=(rc,))
        okc = self.otokc[0]
        P.op("dve", lambda e: e.tensor_tensor(out=okc[:, 0:384].rearrange("p (h d) -> p h d", h=6),
                                             in0=accc_[:, 0:390].rearrange("p (h d) -> p h d", h=6)[:, :, 0:64],
                                             in1=rc[:, 0:6].unsqueeze(2).to_broadcast([128, 6, 64]), op=ALU.mult), r=(accc_, rc), w=(okc,))
        self.norm_into_cat(self.otok[0], self.otok[0][:, 0:384], 384, 0, 256)
        self.norm_into_cat(okc, okc[:, 0:384], 384, 0, 640)

    def sample_layer(self, l):
        P = self.P
        self.prompt_mode = False
        self.wsT_cur = self.wsTs
        self.bs_cur = self.bs_s
        self.sample_params(l)
        self.norm_to_T(self.xsr, "attn", self.xnT, 0)
        outs = {"sbk": self.s_sbk[l], "sbv": self.s_sbv[l], "fk": self.s_fk[l], "fv": self.s_fv[l], "cv": self.s_cv[l]}
        self.inproj(l, NG, [NT], 128, outs)
        P.dma("sp", None, self.s_lf[l], self.lfo, self.lfo[:, 0, :])
        P.handoff([self.hT], self.hviews)
        self.attn_sample(l)
        P.handoff(self.hviews, [self.hT])
        self.merge_wo(l, 1, 128, [self.xsr])
        self.ffn(l, 1, 128, [self.xsr])
        if l == L - 1:
            self.final_norm(self.xsr, self.y_s)

    def merge_wo(self, l, nt, ntok, xbufs):
        P = self.P
        for i in range(nt):
            self.to_T(self.cat[i], "mix", self.xnT, 128 * i)
        for cc in range(2):
            wb = self.load_w(self.w_o[l][:, 512 * cc:512 * (cc + 1)], 512)
            for i in range(nt):
                ps = self.ps()
                self.mm_tok(ps, 512, self.xnT, 128 * i, wb, 0)
                xb = xbufs[i]
                P.op("dve", lambda e, xb=xb, ps=ps, cc=cc: e.tensor_tensor(out=xb[:, 512 * cc:512 * (cc + 1)], in0=xb[:, 512 * cc:512 * (cc + 1)],
                                                                            in1=ps[:], op=ALU.add), r=(xb, ps), w=(xb,))

    def ffn(self, l, nt, ntok, xbufs):
        P = self.P
        for i in range(nt):
            self.norm_to_T(xbufs[i], "ffn", self.xnT, 128 * i)
        AB = (0, 1, 2, 3, 4, 5, 6, 7)
        P.handoff(self.wbuf, self.wh)
        for jb in range(NJ // 2):
            ws = []
            for c0 in (256 * jb, DFF + 256 * jb):
                wb = self.nxt("wh", self.wh)
                P.dma("pool", wb, wb[:, :, :], None, self.w_fi[l][:, c0:c0 + 256].rearrange("(c p) n -> p c n", p=128))
                ws.append(wb)
            wg, wu = ws
            for jj in range(2):
                j = 2 * jb + jj
                pg = self.ps(AB)
                self.mm_feat(pg, 128, wg, 128 * jj, self.xnT, ntok)
                pu = self.ps(AB)
                self.mm_feat(pu, 128, wu, 128 * jj, self.xnT, ntok)
                sg = self.nxt("t512", self.t512)
                P.op("act", lambda e, pg=pg, sg=sg: e.activation(out=sg[:, 0:ntok], in_=pg[:, 0:ntok], func=AF.Silu), r=(pg,), w=(sg,))
                P.op("dve", lambda e, pu=pu, sg=sg, j=j: e.tensor_tensor(out=self.hT[:, j, 0:ntok], in0=sg[:, 0:ntok], in1=pu[:, 0:ntok], op=ALU.mult),
                     r=(sg, pu), w=(self.hT,))
        P.handoff(self.wh, self.wbuf)
        slots = self.wob + self.cat
        for j in range(NJ):
            wo = self.nxt("wob", slots)
            P.dma("pool", wo, wo[:], None, self.w_fo[l][128 * j:128 * (j + 1), :])
            for i in range(nt):
                for cc in range(2):
                    ps = self.psb[2 * i + cc]
                    P.mm(ps, ps[:], self.hT, self.hT[:, j, 128 * i:128 * (i + 1)], wo, wo[:, 512 * cc:512 * (cc + 1)],
                         start=(j == 0), stop=(j == NJ - 1))
        for i in range(nt):
            for cc in range(2):
                ps = self.psb[2 * i + cc]
                xb = xbufs[i]
                P.op("dve", lambda e, xb=xb, ps=ps, cc=cc: e.tensor_tensor(out=xb[:, 512 * cc:512 * (cc + 1)], in0=xb[:, 512 * cc:512 * (cc + 1)],
                                                                            in1=ps[:], op=ALU.add), r=(xb, ps), w=(xb,))

    def final_norm(self, xb, dram_ap):
        P = self.P
        if self.gfin is None:
            self.gfin = P.sb("gfin", [128, D], F32)
            P.dma("sp", self.gfin, self.gfin[:], None, self.g_final.partition_broadcast(128))
        sm = self.rms_stats(xb, xb[:], D)
        P.op("dve", lambda e: e.scalar_tensor_tensor(out=xb[:], in0=xb[:], scalar=sm[:, 0:1], in1=self.gfin[:], op0=ALU.mult, op1=ALU.mult),
             r=(xb, sm, self.gfin), w=(xb,))
        P.dma("sp", None, dram_ap, xb, xb[:])

    def prompt_group(self, l, g):
        P = self.P
        self.prompt_mode = True
        self.wsT_cur = self.wsT
        self.bs_cur = self.bs_b
        tiles = [GT * g + i for i in range(GT)]
        src = self.xp if l == 0 else self.xscr
        for i, t in enumerate(tiles):
            P.dma("sp", self.xg[i], self.xg[i][:], None if l == 0 else self.xscr_b[t], src[128 * t:128 * (t + 1), :])
        for i in range(GT):
            self.norm_to_T(self.xg[i], "attn", self.xnT, 128 * i)
        r0, r1 = 512 * g, 512 * (g + 1)
        outs = {"sbk": self.o_sbk[l, r0:r1, :], "sbv": self.o_sbv[l, r0:r1, :], "fk": self.o_fk[l, r0:r1, :],
                "fv": self.o_fv[l, r0:r1, :], "cv": None}
        import os
        stage = int(os.environ.get("STAGE", "9"))
        if stage >= 1:
            self.inproj(l, g, tiles, 512, outs)
            P.dma("sp", None, self.o_lf[l, r0:r1, :].rearrange("(i p) h -> p i h", p=128), self.lfo, self.lfo[:])
        if stage >= 2:
            P.handoff([self.hT], self.hviews)
            self.attn_prompt(g)
            P.handoff(self.hviews, [self.hT])
        if stage >= 3:
            self.merge_wo(l, GT, 512, self.xg)
        if stage >= 4:
            self.ffn(l, GT, 512, self.xg)
        for i, t in enumerate(tiles):
            if l == L - 1:
                self.final_norm(self.xg[i], self.y_p[128 * t:128 * (t + 1), :])
            else:
                P.dma("sp", self.xscr_b[t], self.xscr[128 * t:128 * (t + 1), :], self.xg[i], self.xg[i][:])

    def build(self):
        import os
        nl = int(os.environ.get("NLAYER", L))
        ng = int(os.environ.get("NGROUP", NG))
        if self.with_sample:
            self.sample_setup()
        for l in range(nl):
            self.layer_params(l)
            for g in range(ng):
                self.prompt_group(l, g)
            if self.with_sample:
                self.sample_layer(l)
        print("sbuf left", self.nc.sbuf_bytes_remaining)
        print("ops", {e: len(self.P.ops[e]) for e in ENGS}, "dma", self.P.ndma)
        self.P.finalize()


_CACHE = {}


def build_nc(with_sample):
    nc = bass.Bass("TRN2", target_bir_lowering=False)
    with ExitStack() as st:
        k = K(nc, st, with_sample)
        k.build()
    return nc


def kernel(**inp):
    f32 = lambda a: np.ascontiguousarray(np.asarray(a), dtype=np.float32)
    nc = build_nc(True)
    consts = host_consts()
    shared = {"w_in": f32(inp["w_in"]), "w_o": f32(inp["w_o"]), "w_ffn_in": f32(inp["w_ffn_in"]),
              "w_ffn_out": f32(inp["w_ffn_out"]), "g_attn": f32(inp["g_attn"]), "g_mix": f32(inp["g_mix"]),
              "g_ffn": f32(inp["g_ffn"]), "g_final": f32(inp["g_final"]), "g_v": f32(inp["g_v"]), "b_f": f32(inp["b_f"]),
              "w_s": f32(inp["w_s"]), "b_s": f32(inp["b_s"])}
    shared.update(consts)
    c_all = np.empty((L * NPOOL * 128, 1542), np.float32)
    for k0, nm in ((0, "cache_sb_k"), (384, "cache_sb_v"), (768, "cache_fox_k"), (1152, "cache_fox_v")):
        c_all[:, k0:k0 + 384] = np.asarray(inp[nm], dtype=np.float32).reshape(-1, 384)
    c_all[:, 1536:1542] = np.asarray(inp["cache_fox_logf"], dtype=np.float32).reshape(-1, 6)
    shared["c_all"] = c_all
    xp = f32(inp["x_prompt"])
    xs = f32(inp["x_sample"])
    pt = np.ascontiguousarray(np.asarray(inp["page_table"]), dtype=np.int32)
    in_maps = []
    for c in range(8):
        m = dict(shared)
        m["xp"] = xp[c]
        m["xs"] = xs[16 * c:16 * (c + 1)].reshape(128, D)
        m["pt"] = pt[16 * c:16 * (c + 1)].reshape(1, 256)
        in_maps.append(m)
    res = run_bass_kernel_spmd(nc, in_maps, core_ids=list(range(8))).results
    y_p = np.stack([r["y_p"] for r in res])

    def pk(n):
        return np.stack([r[n] for r in res], axis=1).reshape(L, 8, S, 6, 64)

    def sk(n, tail):
        return np.concatenate([r[n].reshape(L, 16, 8, *tail) for r in res], axis=1)

    p_lf = np.stack([r["o_lf"] for r in res], axis=1)
    y_s = np.concatenate([r["y_s"].reshape(16, 8, D) for r in res], axis=0)
    return (y_p, y_s, pk("o_sbk"), pk("o_sbv"), pk("o_fk"), pk("o_fv"), p_lf,
            sk("s_sbk", (6, 64)), sk("s_sbv", (6, 64)), sk("s_fk", (6, 64)), sk("s_fv", (6, 64)), sk("s_lf", (6,)),
            sk("s_cv", (256,)))
```

```python
import numpy as np
import ml_dtypes
from contextlib import ExitStack
import concourse.bass as bass
import concourse.mybir as mybir
from concourse.bass_utils import run_bass_kernel_spmd

F32 = mybir.dt.float32
BF16 = mybir.dt.bfloat16
I32 = mybir.dt.int32
AF = mybir.ActivationFunctionType
ALU = mybir.AluOpType

ENGS = ("pe", "act", "dve", "pool", "sp")
EPOCH = 6000
NDSEM = 12

D = 1024
S = 2048
NT = 16
GT = 4
NG = NT // GT
NIN = 2822
DFF = 2816
NJ = 22
L = 2
NPOOL = 2560
NEGM = -30000.0


class Buf:
    __slots__ = ("name", "t", "lw", "rd")

    def __init__(self, name, t):
        self.name = name
        self.t = t
        self.lw = None
        self.rd = []

    def __getitem__(self, k):
        return self.t[k]


class Prog:
    def __init__(self, nc, stack):
        self.nc = nc
        self.stack = stack
        self.ops = {e: [] for e in ENGS}
        self.seen_c = {e: {e2: -1 for e2 in ENGS} for e in ENGS}
        self.seen_d = {e: set() for e in ENGS}
        self.ndma = {e: 0 for e in ENGS}
        self.dtok = {e: [] for e in ENGS}
        self.names = set()

    def sb(self, name, shape, dt):
        assert name not in self.names, name
        self.names.add(name)
        t = self.stack.enter_context(self.nc.sbuf_tensor(name, list(shape), dt))
        return Buf(name, t)

    def ps(self, name, shape, dt):
        t = self.stack.enter_context(self.nc.psum_tensor(name, list(shape), dt))
        return Buf(name, t)

    def op(self, eng, fn, r=(), w=(), dma=False):
        ops = self.ops[eng]
        idx = len(ops)
        deps = set()
        for b in r:
            if b.lw is not None:
                deps.add(b.lw)
        for b in w:
            if b.lw is not None:
                deps.add(b.lw)
            for t in b.rd:
                deps.add(t)
        waits = []
        if dma:
            d = self.ndma[eng]
            self.ndma[eng] += 1
            tok = ("d", eng, d)
            self.dtok[eng].append(tok)
            if d >= NDSEM:
                deps.add(("d", eng, d - NDSEM))
        else:
            tok = ("c", eng, idx)
        cmax = {}
        for t in deps:
            if t[0] == "c":
                _, e2, i2 = t
                if e2 == eng and eng == "pe":
                    continue
                if i2 <= self.seen_c[eng][e2]:
                    continue
                if i2 > cmax.get(e2, -1):
                    cmax[e2] = i2
            else:
                if t in self.seen_d[eng]:
                    continue
                self.seen_d[eng].add(t)
                waits.append(t)
        for e2, i2 in cmax.items():
            self.seen_c[eng][e2] = i2
            waits.append(("c", e2, i2))
        ops.append({"fn": fn, "waits": waits, "dma": dma, "sig": False, "tok": tok})
        for b in r:
            b.rd.append(tok)
            if len(b.rd) > 24:
                b.rd = self._compact(b.rd)
        for b in w:
            b.lw = tok
            b.rd = []
        return tok

    @staticmethod
    def _compact(rd):
        best = {}
        out = []
        for t in rd:
            if t[0] == "c":
                if t[2] > best.get(t[1], -1):
                    best[t[1]] = t[2]
            else:
                out.append(t)
        return out + [("c", e, i) for e, i in best.items()]

    @staticmethod
    def handoff(srcs, dsts):
        toks = []
        for b in srcs:
            if b.lw is not None:
                toks.append(b.lw)
            toks.extend(b.rd)
        for d in dsts:
            d.rd = list(d.rd) + toks

    def finalize(self):
        nc = self.nc
        for e in ENGS:
            for o in self.ops[e]:
                for t in o["waits"]:
                    if t[0] == "c":
                        self.ops[t[1]][t[2]]["sig"] = True
        csem = {}
        for e in ENGS:
            n = 0
            for o in self.ops[e]:
                if o["sig"] and not o["dma"]:
                    ep, v = divmod(n, EPOCH)
                    o["sv"] = (ep, v + 1)
                    n += 1
            nep = (n + EPOCH - 1) // EPOCH
            csem[e] = [self.stack.enter_context(nc.semaphore(f"c_{e}_{k}")) for k in range(nep)]
        dsem = {}
        for e in ENGS:
            k = min(self.ndma[e], NDSEM)
            dsem[e] = [self.stack.enter_context(nc.semaphore(f"d_{e}_{j}")) for j in range(k)]

        def resolve(t):
            if t[0] == "c":
                o = self.ops[t[1]][t[2]]
                ep, v = o["sv"]
                return csem[t[1]][ep], v
            _, e2, d = t
            return dsem[e2][d % NDSEM], 16 * (d // NDSEM + 1)

        def run(ename, eng):
            for o in self.ops[ename]:
                for t in o["waits"]:
                    s, v = resolve(t)
                    eng.wait_ge(s, v)
                ins = o["fn"](eng)
                if o["dma"]:
                    s, v = resolve(o["tok"])
                    ins.then_inc(s, 16)
                elif o["sig"]:
                    ep, v = o["sv"]
                    ins.then_inc(csem[ename][ep], 1)
            for t in self.dtok[ename][-NDSEM:]:
                s, v = resolve(t)
                eng.wait_ge(s, v)

        with nc.Block() as block:
            @block.tensor
            def _(e):
                run("pe", e)

            @block.scalar
            def _(e):
                run("act", e)

            @block.vector
            def _(e):
                run("dve", e)

            @block.gpsimd
            def _(e):
                run("pool", e)

            @block.sync
            def _(e):
                run("sp", e)

    def mm(self, out, out_ap, lhsT, lhsT_ap, rhs, rhs_ap, start=True, stop=True):
        return self.op("pe", lambda e: e.matmul(out_ap, lhsT_ap, rhs_ap, start=start, stop=stop),
                       r=(lhsT, rhs), w=(out,))

    def tr(self, out, out_ap, in_, in_ap, ident, ident_ap):
        return self.op("pe", lambda e: e.transpose(out_ap, in_ap, ident_ap), r=(in_, ident), w=(out,))

    def dma(self, eng, out, out_ap, in_, in_ap, **kw):
        r = (in_,) if in_ is not None else ()
        w = (out,) if out is not None else ()
        return self.op(eng, lambda e: e.dma_start(out=out_ap, in_=in_ap, **kw), r=r, w=w, dma=True)


def host_consts():
    j = np.arange(128)
    bf = ml_dtypes.bfloat16
    c = {}
    c["c_ident"] = np.eye(128).astype(bf)
    c["c_identf"] = np.eye(128, dtype=np.float32)
    c["c_negU"] = (-(j[:, None] >= j[None, :]).astype(np.float32)).astype(bf)
    c["c_negones"] = (-np.ones((128, 128), np.float32)).astype(bf)
    c["c_utrif"] = (j[:, None] <= j[None, :]).astype(np.float32)
    c["c_onesf"] = np.ones((128, 128), np.float32)
    c["c_mstrict"] = np.where(j[:, None] < j[None, :], 0.0, NEGM).astype(bf)
    c["c_mincl"] = np.where(j[:, None] <= j[None, :], 0.0, NEGM).astype(bf)
    sel = np.zeros((70, 6, 128), np.float32)
    for h in range(6):
        sel[h, h, :] = 1
        sel[32 + h, h, :] = 1
        sel[64 + h, h, :] = 1
    c["c_sel"] = sel.astype(bf)
    c["c_tril"] = (j[None, :] <= j[:, None]).astype(np.float32)
    same = (j[:, None] // 8) == (j[None, :] // 8)
    c["c_mblk_s"] = np.where(same & (j[:, None] % 8 < j[None, :] % 8), 0.0, NEGM).astype(bf)
    c["c_mblk_i"] = np.where(same & (j[:, None] % 8 <= j[None, :] % 8), 0.0, NEGM).astype(bf)
    c["c_bcum"] = (same & (j[:, None] <= j[None, :])).astype(np.float32)
    c["c_ugt"] = (j[:, None] > j[None, :]).astype(np.float32)
    return c


class K:
    def __init__(self, nc, st, with_sample):
        self.nc = nc
        self.P = Prog(nc, st)
        self.with_sample = with_sample
        self.cnt = 0
        P = self.P
        dt = lambda n, s, d, k="ExternalInput": nc.dram_tensor(n, list(s), d, kind=k).ap()
        self.xp = dt("xp", [S, D], F32)
        self.w_in = dt("w_in", [L, D, NIN], F32)
        self.w_o = dt("w_o", [L, D, D], F32)
        self.w_fi = dt("w_ffn_in", [L, D, 2 * DFF], F32)
        self.w_fo = dt("w_ffn_out", [L, DFF, D], F32)
        self.g_attn = dt("g_attn", [L, D], F32)
        self.g_mix = dt("g_mix", [L, D], F32)
        self.g_ffn = dt("g_ffn", [L, D], F32)
        self.g_final = dt("g_final", [D], F32)
        self.g_v = dt("g_v", [L, 256], F32)
        self.b_f = dt("b_f", [L, 6], F32)
        self.w_s = dt("w_s", [L, 4, 128, 128], F32)
        self.b_s = dt("b_s", [L, 4, 128], F32)
        self.cd = {}
        for n, a in host_consts().items():
            self.cd[n] = dt(n, a.shape, BF16 if a.dtype != np.float32 else F32)
        self.y_p = dt("y_p", [S, D], F32, "ExternalOutput")
        self.o_sbk = dt("o_sbk", [L, S, 384], F32, "ExternalOutput")
        self.o_sbv = dt("o_sbv", [L, S, 384], F32, "ExternalOutput")
        self.o_fk = dt("o_fk", [L, S, 384], F32, "ExternalOutput")
        self.o_fv = dt("o_fv", [L, S, 384], F32, "ExternalOutput")
        self.o_lf = dt("o_lf", [L, S, 6], F32, "ExternalOutput")
        self.xscr = dt("xscr", [S, D], F32, "ExternalOutput")
        self.xscr_b = [Buf(f"xscr{t}", None) for t in range(NT)]

        sb = P.sb
        self.ident = sb("ident", [128, 128], BF16)
        self.identf = sb("identf", [128, 128], F32)
        self.negU = sb("negU", [128, 128], BF16)
        self.negones = sb("negones", [128, 128], BF16)
        self.utrif = sb("utrif", [128, 128], F32)
        self.onesf = sb("onesf", [128, 128], F32)
        self.mstrict = sb("mstrict", [128, 128], BF16)
        self.mincl = sb("mincl", [128, 128], BF16)
        self.sel = sb("sel", [70, 6, 128], BF16)
        self.tril = sb("tril", [128, 128], F32)
        for b, n in ((self.ident, "c_ident"), (self.identf, "c_identf"), (self.negU, "c_negU"),
                     (self.negones, "c_negones"), (self.utrif, "c_utrif"), (self.onesf, "c_onesf"),
                     (self.mstrict, "c_mstrict"), (self.mincl, "c_mincl"), (self.sel, "c_sel"),
                     (self.tril, "c_tril")):
            P.dma("sp", b, b[:], None, self.cd[n])
        self.one = sb("one", [128, 1], F32)
        self.eps = sb("eps", [128, 1], F32)
        self.zt = sb("zt", [128, 512], BF16)
        P.op("dve", lambda e: e.memset(self.one[:], 1.0), w=(self.one,))
        P.op("dve", lambda e: e.memset(self.eps[:], 1e-6), w=(self.eps,))
        P.op("dve", lambda e: e.memset(self.zt[:], 0.0), w=(self.zt,))
        self.gcol = {k: sb("gcol_" + k, [128, 8], F32) for k in ("attn", "mix", "ffn")}
        self.gv_b = sb("gv_b", [128, 256], F32)
        self.bf_b = sb("bf_b", [128, 6], F32)
        self.bs_col = sb("bs_col", [128, 4], F32)
        self.bs_b = sb("bs_b", [128, 4, 64], F32)
        self.wsT = [sb(f"wsT{g}", [128, 128], BF16) for g in range(4)]
        self.wstmp = sb("wstmp", [128, 128], F32)
        self.psb = [P.ps(f"ps{i}", [128, 512], F32) for i in range(8)]
        self.psi = 0
        self.kbT = [[sb(f"kbT{p}_{g}", [128, 512 if g < NG else 128], BF16) for g in range(NG + 1)] for p in range(3)]
        self.kcT = [[sb(f"kcT{p}_{g}", [128, 512 if g < NG else 128], BF16) for g in range(NG + 1)] for p in range(3)]
        self.vb = [sb(f"vb{t}", [128, 384], BF16) for t in range(NT + 1)]
        self.vc = [sb(f"vc{t}", [128, 6, 65], BF16) for t in range(NT + 1)]
        for t in range(NT + 1):
            P.op("pool", lambda e, t=t: e.memset(self.vc[t][:], 1.0), w=(self.vc[t],))
        self.lfh = [sb(f"lfh{t}", [128, 6], F32) for t in range(NT + 1)]
        self.lfrep = [sb(f"lfrep{t}", [128, 96], F32) for t in range(NT + 1)]
        for t in range(NT + 1):
            P.op("pool", lambda e, t=t: e.memset(self.lfrep[t][:], 0.0), w=(self.lfrep[t],))
        self.negF = [sb(f"negF{t}", [128, 6], F32) for t in range(NT + 1)]
        self.xg = [sb(f"xg{i}", [128, D], F32) for i in range(GT)]
        self.xnT = sb("xnT", [128, 8, 512], BF16)
        self.qbT = [sb(f"qbT{p}", [128, 512], BF16) for p in range(3)]
        self.qcT = [sb(f"qcT{p}", [128, 512], BF16) for p in range(3)]
        self.Fp = sb("Fp", [70, 512], BF16)
        P.op("pool", lambda e: e.memset(self.Fp[:], 0.0), w=(self.Fp,))
        self.cat = [sb(f"cat{i}", [128, D], BF16) for i in range(GT)]
        self.otok = [sb(f"otok{i}", [128, 384], F32) for i in range(GT)]
        self.hT = sb("hT", [128, NJ, 512], BF16)
        def hview(k0, nk, dt_, name):
            ap = self.hT.t[:, k0:k0 + nk, :].rearrange("p a b -> p (a b)")
            return Buf(name, ap.bitcast(dt_) if dt_ != BF16 else ap)
        self.otokc = [hview(2 * i, 2, F32, f"otokc{i}") for i in range(GT)]
        self.gstag = [hview(k0, 7, F32, f"gstag{k0}") for k0 in (0, 8, 15)]
        self.OT = [hview(8 + 2 * i, 2, F32, f"OT{i}") for i in range(2)]
        self.hviews = self.otokc + self.gstag + self.OT
        self.wbuf = [sb(f"wbuf{i}", [128, 8, 512], BF16) for i in range(2)]
        self.wi = 0
        self.wh = [Buf(f"wh{i}_{k}", self.wbuf[i].t[:, :, 256 * k:256 * (k + 1)]) for i in range(2) for k in range(2)]
        self.wob = [sb(f"wob{i}", [128, D], BF16) for i in range(2)]
        self.woi = 0
        self.t512 = [sb(f"t512_{i}", [128, 512], F32) for i in range(3)]
        self.t512i = 0
        self.stg = [sb(f"stg{i}", [128, 392], F32) for i in range(4)]
        self.stgi = 0
        self.xs16 = [sb(f"xs16_{i}", [128, D], BF16) for i in range(2)]
        self.xs16i = 0
        self.sm = [sb(f"sm{i}", [128, 8], F32) for i in range(6)]
        self.smi = 0
        self.ebuf = [sb(f"ebuf{i}", [128, 512], F32) for i in range(2)]
        self.lbuf = [sb(f"lbuf{i}", [128, 512], BF16) for i in range(2)]
        self.abuf = [sb(f"abuf{i}", [128, 512], BF16) for i in range(2)]
        self.abufc = [sb(f"abufc{i}", [128, 512], BF16) for i in range(2)]
        self.acc = sb("acc", [128, 512], F32)
        self.accb = [sb(f"accb{i}", [128, 512], BF16) for i in range(2)]
        self.rot = {}
        self.lfo = sb("lfo", [128, GT, 6], F32)
        self.fpt = [sb(f"fpt{i}", [70, 128], F32) for i in range(2)]
        self.fpb = [sb(f"fpb{i}", [70, 128], BF16) for i in range(2)]
        self.gfin = None

    def nxt(self, key, lst):
        i = self.rot.get(key, 0)
        self.rot[key] = i + 1
        return lst[i % len(lst)]

    def ps(self, banks=(0, 1, 2, 3, 4, 5, 6, 7)):
        i = self.rot.get(("ps", banks), 0)
        self.rot[("ps", banks)] = i + 1
        return self.psb[banks[i % len(banks)]]

    def small(self):
        return self.nxt("sm", self.sm)

    def layer_params(self, l):
        P = self.P
        for k, src in (("attn", self.g_attn), ("mix", self.g_mix), ("ffn", self.g_ffn)):
            P.dma("sp", self.gcol[k], self.gcol[k][:], None, src[l].rearrange("(c p) -> p c", p=128),
                  allow_slow_non_contiguous=True)
        P.dma("sp", self.gv_b, self.gv_b[:], None, self.g_v[l].partition_broadcast(128))
        P.dma("sp", self.bf_b, self.bf_b[:], None, self.b_f[l].partition_broadcast(128))
        P.dma("sp", self.bs_col, self.bs_col[:], None, self.b_s[l].rearrange("g t -> t g"),
              allow_slow_non_contiguous=True)
        P.op("dve", lambda e: e.tensor_copy(out=self.bs_b[:], in_=self.bs_col[:, :].unsqueeze(2).to_broadcast([128, 4, 64])),
             r=(self.bs_col,), w=(self.bs_b,))
        for g in range(4):
            P.dma("sp", self.wstmp, self.wstmp[:], None, self.w_s[l, g])
            P.op("dve", lambda e: e.tensor_tensor(out=self.wstmp[:], in0=self.wstmp[:], in1=self.tril[:], op=ALU.mult),
                 r=(self.wstmp, self.tril), w=(self.wstmp,))
            ps = self.ps()
            P.tr(ps, ps[:, 0:128], self.wstmp, self.wstmp[:], self.identf, self.identf[:])
            P.op("dve", lambda e, g=g, ps=ps: e.tensor_copy(out=self.wsT[g][:], in_=ps[:, 0:128]), r=(ps,), w=(self.wsT[g],))

    def load_w(self, src_ap, ncols):
        wb = self.nxt("wbuf", self.wbuf)
        self.P.dma("pool", wb, wb[:, :, 0:ncols], None, src_ap.rearrange("(c p) n -> p c n", p=128))
        return wb

    def rms_stats(self, src_buf, src_ap, n):
        P = self.P
        sm = self.small()
        P.op("dve", lambda e: e.memset(sm[:], 0.0), w=(sm,))
        junk = self.nxt("t512", self.t512)
        w = src_ap.shape[-1]
        ncol = 0
        for c0 in range(0, w, 512):
            c1 = min(w, c0 + 512)
            P.op("act", lambda e, c0=c0, c1=c1, k=ncol: e.activation(out=junk[:, 0:c1 - c0], in_=src_ap[:, c0:c1], func=AF.Square,
                                                                      accum_out=sm[:, 1 + k:2 + k]),
                 r=(src_buf,), w=(junk, sm))
            ncol += 1
        if ncol > 1:
            P.op("dve", lambda e: e.tensor_tensor(out=sm[:, 1:2], in0=sm[:, 1:2], in1=sm[:, 2:3], op=ALU.add), r=(sm,), w=(sm,))
        P.op("act", lambda e: e.activation(out=sm[:, 0:1], in_=sm[:, 1:2], func=AF.Sqrt, scale=1.0 / n, bias=self.eps[:]),
             r=(sm, self.eps), w=(sm,))
        P.op("dve", lambda e: e.reciprocal(out=sm[:, 0:1], in_=sm[:, 0:1]), r=(sm,), w=(sm,))
        return sm

    def norm_to_T(self, xbuf, gkey, dstT, col0):
        P = self.P
        sm = self.rms_stats(xbuf, xbuf[:], D)
        xs = self.nxt("xs16", self.xs16)
        P.op("dve", lambda e: e.tensor_scalar(out=xs[:], in0=xbuf[:], scalar1=sm[:, 0:1], scalar2=None, op0=ALU.mult),
             r=(xbuf, sm), w=(xs,))
        self.to_T(xs, gkey, dstT, col0)

    def to_T(self, xs, gkey, dstT, col0):
        P = self.P
        ps = self.ps()
        pb = ps[:].bitcast(BF16)
        for c in range(8):
            P.tr(ps, pb[:, c * 128:(c + 1) * 128], xs, xs[:, c * 128:(c + 1) * 128], self.ident, self.ident[:])
        g = self.gcol[gkey]
        P.op("dve", lambda e: e.tensor_tensor(out=dstT[:, :, col0:col0 + 128], in0=pb.rearrange("p (c n) -> p c n", c=8),
                                             in1=g[:, :].unsqueeze(2).to_broadcast([128, 8, 128]), op=ALU.mult),
             r=(ps, g), w=(dstT,))

    def mm_tok(self, ps, ncols, xT, col0, wb, wc0):
        for c in range(8):
            self.P.mm(ps, ps[:, 0:ncols], xT, xT[:, c, col0:col0 + 128], wb, wb[:, c, wc0:wc0 + ncols],
                      start=(c == 0), stop=(c == 7))

    def mm_feat(self, ps, nrows, wb, wc0, xT, ntok):
        for c in range(8):
            self.P.mm(ps, ps[0:nrows, 0:ntok], wb, wb[:, c, wc0:wc0 + nrows], xT, xT[:, c, 0:ntok],
                      start=(c == 0), stop=(c == 7))

    def inproj(self, l, g, tiles, ntok, outs):
        P = self.P
        import os
        self.ipmax = int(os.environ.get("IPMAX", 99))
        nt = len(tiles)
        xT = self.xnT
        if self.ipmax <= 0:
            return
        wb = self.load_w(self.w_in[l][:, 0:512], 512)
        for i in range(nt):
            ps = self.ps()
            self.mm_tok(ps, 512, xT, 128 * i, wb, 0)
            self.gate(ps, i, outs)
        if self.ipmax <= 1:
            return
        wb = self.load_w(self.w_in[l][:, 512:896], 384)
        for p in range(3):
            ps = self.ps()
            self.mm_feat(ps, 128, wb, 128 * p, xT, ntok)
            P.op("act", lambda e, p=p, ps=ps: e.activation(out=self.qbT[p][:, 0:ntok], in_=ps[:, 0:ntok], func=AF.Copy, scale=0.125),
                 r=(ps,), w=(self.qbT[p],))
        if self.ipmax <= 2:
            return
        wb = self.load_w(self.w_in[l][:, 896:1280], 384)
        for p in range(3):
            ps = self.ps()
            self.mm_feat(ps, 128, wb, 128 * p, xT, ntok)
            dst = self.kT_dst("b", p, g)
            P.op("dve", lambda e, ps=ps, dst=dst: e.tensor_copy(out=dst[:, 0:ntok], in_=ps[:, 0:ntok]), r=(ps,), w=(dst,))
        for i in range(nt):
            ps = self.ps()
            self.mm_tok(ps, 384, xT, 128 * i, wb, 0)
            self.out_tok(ps, outs["sbk"], i, None)
        if self.ipmax <= 3:
            return
        wb = self.load_w(self.w_in[l][:, 1280:1664], 384)
        for i in range(nt):
            ps = self.ps()
            self.mm_tok(ps, 384, xT, 128 * i, wb, 0)
            self.out_tok(ps, outs["sbv"], i, ("vb", tiles[i]))
        if self.ipmax <= 4:
            return
        wb = self.load_w(self.w_in[l][:, 1664:2048], 384)
        for p in range(3):
            ps = self.ps()
            self.mm_feat(ps, 128, wb, 128 * p, xT, ntok)
            P.op("act", lambda e, p=p, ps=ps: e.activation(out=self.qcT[p][:, 0:ntok], in_=ps[:, 0:ntok], func=AF.Copy, scale=0.125),
                 r=(ps,), w=(self.qcT[p],))
        if self.ipmax <= 5:
            return
        wb = self.load_w(self.w_in[l][:, 2048:2432], 384)
        for p in range(3):
            ps = self.ps()
            self.mm_feat(ps, 128, wb, 128 * p, xT, ntok)
            dst = self.kT_dst("c", p, g)
            P.op("dve", lambda e, ps=ps, dst=dst: e.tensor_copy(out=dst[:, 0:ntok], in_=ps[:, 0:ntok]), r=(ps,), w=(dst,))
        for i in range(nt):
            ps = self.ps()
            self.mm_tok(ps, 384, xT, 128 * i, wb, 0)
            self.out_tok(ps, outs["fk"], i, None)
        if self.ipmax <= 6:
            return
        wb = self.load_w(self.w_in[l][:, 2432:2822], 390)
        for i in range(nt):
            ps = self.ps()
            self.mm_tok(ps, 390, xT, 128 * i, wb, 0)
            st = self.out_tok(ps, outs["fv"], i, ("vc", tiles[i]), ncols=390)
            self.logf(st, i, tiles[i], outs)

    def kT_dst(self, kind, p, g):
        return (self.kbT if kind == "b" else self.kcT)[p][g]

    def out_tok(self, ps, dram_ap, i, keep, ncols=384):
        P = self.P
        st = self.nxt("stg", self.stg)
        P.op("act", lambda e: e.activation(out=st[:, 0:ncols], in_=ps[:, 0:ncols], func=AF.Copy), r=(ps,), w=(st,))
        P.dma("sp", None, dram_ap[128 * i:128 * (i + 1), :], st, st[:, 0:384])
        if keep is not None:
            kind, t = keep
            if kind == "vb":
                dst = self.vb_dst(t)
                P.op("dve", lambda e: e.tensor_copy(out=dst[:], in_=st[:, 0:384]), r=(st,), w=(dst,))
            else:
                dst = self.vc_dst(t)
                P.op("dve", lambda e: e.tensor_copy(out=dst[:, :, 0:64], in_=st[:, 0:384].rearrange("p (h d) -> p h d", h=6)),
                     r=(st,), w=(dst,))
        return st

    def vb_dst(self, t):
        return self.vb[t]

    def vc_dst(self, t):
        return self.vc[t]

    def gate(self, ps, i, outs):
        P = self.P
        import os
        if os.environ.get("NOGATE"):
            return
        uv = self.nxt("t512", self.t512)
        P.op("act", lambda e: e.activation(out=uv[:], in_=ps[:], func=AF.Gelu_apprx_tanh), r=(ps,), w=(uv,))
        sm = self.rms_stats(uv, uv[:, 256:512], 256)
        vn32 = self.nxt("stg", self.stg)
        P.op("dve", lambda e: e.scalar_tensor_tensor(out=vn32[:, 0:256], in0=uv[:, 256:512], scalar=sm[:, 0:1], in1=self.gv_b[:],
                                                     op0=ALU.mult, op1=ALU.mult), r=(uv, sm, self.gv_b), w=(vn32,))
        if outs.get("cv") is not None:
            P.dma("sp", None, outs["cv"][128 * i:128 * (i + 1), :], vn32, vn32[:, 0:256])
        vn = self.nxt("xs16", self.xs16)
        P.op("dve", lambda e: e.tensor_copy(out=vn[:, 0:256], in_=vn32[:, 0:256]), r=(vn32,), w=(vn,))
        pm = self.ps()
        for g in range(4):
            P.mm(pm, pm[:, 64 * g:64 * (g + 1)], self.wsT_cur[g], self.wsT_cur[g][:], vn, vn[:, 64 * g:64 * (g + 1)])
        a = self.nxt("t512", self.t512)
        bs = self.bs_cur
        P.op("dve", lambda e: e.tensor_tensor(out=a[:, 0:256], in0=pm[:, 0:256], in1=bs[:].rearrange("p g d -> p (g d)"), op=ALU.add),
             r=(pm, bs), w=(a,))
        P.op("dve", lambda e: e.tensor_tensor(out=a[:, 0:256], in0=a[:, 0:256], in1=uv[:, 0:256], op=ALU.mult), r=(a, uv), w=(a,))
        self.norm_into_cat(a, a[:, 0:256], 256, i, 0)

    def norm_into_cat(self, buf, ap, n, i, c0):
        P = self.P
        sm = self.rms_stats(buf, ap, n)
        cat = self.cat[i]
        P.op("dve", lambda e: e.tensor_scalar(out=cat[:, c0:c0 + n], in0=ap, scalar1=sm[:, 0:1], scalar2=None, op0=ALU.mult),
             r=(buf, sm), w=(cat,))

    def logf(self, ps, i, t, outs):
        P = self.P
        import os
        if os.environ.get("NOLOGF"):
            return
        sm = self.small()
        P.op("dve", lambda e: e.tensor_tensor(out=sm[:, 0:6], in0=ps[:, 384:390], in1=self.bf_b[:], op=ALU.add), r=(ps, self.bf_b), w=(sm,))
        P.op("act", lambda e: e.activation(out=sm[:, 0:6], in_=sm[:, 0:6], func=AF.Exp, scale=-1.0), r=(sm,), w=(sm,))
        P.op("act", lambda e: e.activation(out=sm[:, 0:6], in_=sm[:, 0:6], func=AF.Ln, bias=self.one[:], scale=1.0), r=(sm, self.one), w=(sm,))
        lf = self.lfh[t]
        P.op("dve", lambda e: e.tensor_scalar(out=lf[:], in0=sm[:, 0:6], scalar1=-1.0, scalar2=None, op0=ALU.mult), r=(sm,), w=(lf,))
        P.op("pool", lambda e: e.tensor_copy(out=self.lfo[:, i, :], in_=lf[:]), r=(lf,), w=(self.lfo,))
        if not self.prompt_mode:
            ps2 = self.ps()
            P.mm(ps2, ps2[:, 0:6], self.bcum, self.bcum[:], lf, lf[:])
            P.op("dve", lambda e: e.tensor_scalar(out=self.negF[t][:], in0=ps2[:, 0:6], scalar1=-1.0, scalar2=None, op0=ALU.mult),
                 r=(ps2,), w=(self.negF[t],))
        if self.prompt_mode:
            lr = self.lfrep[t]
            P.op("dve", lambda e: e.tensor_copy(out=lr[:, 0:96].rearrange("p (a b) -> p a b", a=3)[:, :, 0:6],
                                                in_=lf[:, :].unsqueeze(1).to_broadcast([128, 3, 6])), r=(lf,), w=(lr,))
            ps2 = self.ps()
            for t2 in range(t + 1):
                m = self.utrif if t2 == t else self.onesf
                P.mm(ps2, ps2[:, 0:6], m, m[:], self.lfh[t2], self.lfh[t2][:], start=(t2 == 0), stop=(t2 == t))
            P.op("dve", lambda e: e.tensor_scalar(out=self.negF[t][:], in0=ps2[:, 0:6], scalar1=-1.0, scalar2=None, op0=ALU.mult),
                 r=(ps2,), w=(self.negF[t],))
            ps3 = self.ps()
            for t2 in range(t + 1):
                m = self.utrif if t2 == t else self.onesf
                P.mm(ps3, ps3[0:70, 0:128], self.lfrep[t2], self.lfrep[t2][:, 0:70], m, m[:], start=(t2 == 0), stop=(t2 == t))
            hiA = self.nxt("fpb", self.fpb)
            r1 = self.nxt("fpt", self.fpt)
            midA = self.nxt("fpb", self.fpb)
            r2 = self.nxt("fpt", self.fpt)
            P.op("dve", lambda e: e.tensor_copy(out=hiA[:], in_=ps3[0:70, 0:128]), r=(ps3,), w=(hiA,))
            P.op("dve", lambda e: e.tensor_tensor(out=r1[:], in0=ps3[0:70, 0:128], in1=hiA[:], op=ALU.subtract), r=(ps3, hiA), w=(r1,))
            P.op("dve", lambda e: e.tensor_copy(out=midA[:], in_=r1[:]), r=(r1,), w=(midA,))
            P.op("dve", lambda e: e.tensor_tensor(out=r2[:], in0=r1[:], in1=midA[:], op=ALU.subtract), r=(r1, midA), w=(r2,))
            c0 = 128 * i
            P.op("pool", lambda e: e.tensor_copy(out=self.Fp[0:6, c0:c0 + 128], in_=hiA[0:6, :]), r=(hiA,), w=(self.Fp,))
            P.op("pool", lambda e: e.tensor_copy(out=self.Fp[32:38, c0:c0 + 128], in_=midA[32:38, :]), r=(midA,), w=(self.Fp,))
            P.op("pool", lambda e: e.tensor_copy(out=self.Fp[64:70, c0:c0 + 128], in_=r2[64:70, :]), r=(r2,), w=(self.Fp,))

    def attn_begin(self, kind, N, psO):
        nrow = 64 if kind == "b" else 65
        self.P.mm(psO, psO[0:nrow, 0:N], self.zt, self.zt[:, 0:nrow], self.zt, self.zt[:, 0:N], start=True, stop=False)

    def attn_s1(self, kind, h, qT, q0, N, unit):
        P = self.P
        p, r0 = h // 2, 64 * (h % 2)
        kb, qlo, mask, use_sel = unit
        kT = self.kT_dst(kind, p, kb // GT)
        kc0 = 128 * (kb % GT)

        def zmm(ps, final_stop):
            mms = [(ps[:, qlo:N], kT, kT[r0:r0 + 64, kc0:kc0 + 128], qT, qT[r0:r0 + 64, q0 + qlo:q0 + N])]
            if kind == "c" and use_sel:
                mms.append((ps[:, qlo:N], self.sel, self.sel[:, h, :], self.Fp, self.Fp[:, qlo:N]))
            if mask is not None:
                mb, map_, mw = mask
                mms.append((ps[:, qlo:qlo + mw], self.ident, self.ident[:], mb, map_))
            for n_, (o_ap, lb, l_ap, rb, r_ap) in enumerate(mms):
                P.mm(ps, o_ap, lb, l_ap, rb, r_ap, start=(n_ == 0), stop=(final_stop and n_ == len(mms) - 1))

        c = {"kind": kind, "h": h, "kb": kb, "qlo": qlo, "N": N, "zmm": zmm}
        if kind == "b":
            psZ = self.ps((0, 1))
            zmm(psZ, True)
            e_ = self.nxt("ebuf", self.ebuf)
            l_ = self.nxt("lbuf", self.lbuf)
            P.op("act", lambda e: e.activation(out=e_[:, qlo:N], in_=psZ[:, qlo:N], func=AF.Exp), r=(psZ,), w=(e_,))
            P.op("act", lambda e: e.activation(out=l_[:, qlo:N], in_=e_[:, qlo:N], func=AF.Ln, bias=self.one[:], scale=1.0),
                 r=(e_, self.one), w=(l_,))
            c["l"] = l_
        else:
            psZ = self.ps((6, 7))
            zmm(psZ, True)
            a_ = self.nxt("abufc", self.abufc)
            nf = self.negF[kb]
            P.op("act", lambda e: e.activation(out=a_[:, qlo:N], in_=psZ[:, qlo:N], func=AF.Exp, bias=nf[:, h:h + 1], scale=1.0),
                 r=(psZ, nf), w=(a_,))
            c["a"] = a_
        return c

    def attn_s2(self, c, psO, first, last, st):
        P = self.P
        kind, h, kb, qlo, N = c["kind"], c["h"], c["kb"], c["qlo"], c["N"]
        if kind == "b":
            l_ = c["l"]
            psE = self.ps((2, 3))
            c["zmm"](psE, False)
            P.mm(psE, psE[:, qlo:N], self.negU, self.negU[:], l_, l_[:, qlo:N], start=False, stop=first)
            if not first:
                ab_ = st["accb"]
                P.mm(psE, psE[:, qlo:N], self.negones, self.negones[:], ab_, ab_[:, qlo:N], start=False, stop=True)
            a_ = self.nxt("abuf", self.abuf)
            P.op("act", lambda e: e.activation(out=a_[:, qlo:N], in_=psE[:, qlo:N], func=AF.Exp), r=(psE,), w=(a_,))
            if not last:
                if first:
                    P.op("dve", lambda e: e.memset(self.acc[:, 0:N], 0.0), w=(self.acc,))
                P.op("dve", lambda e: e.tensor_tensor(out=self.acc[:, qlo:N], in0=self.acc[:, qlo:N], in1=l_[:, qlo:N], op=ALU.add),
                     r=(self.acc, l_), w=(self.acc,))
                nb = self.nxt("accb", self.accb)
                P.op("dve", lambda e: e.tensor_copy(out=nb[:, 0:N], in_=self.acc[:, 0:N]), r=(self.acc,), w=(nb,))
                st["accb"] = nb
            vt = self.vb[kb]
            P.mm(psO, psO[0:64, qlo:N], vt, vt[:, 64 * h:64 * h + 64], a_, a_[:, qlo:N], start=False, stop=last)
        else:
            a_ = c["a"]
            vt = self.vc[kb]
            P.mm(psO, psO[0:65, qlo:N], vt, vt[:, h, :], a_, a_[:, qlo:N], start=False, stop=last)

    def o_to_tok(self, kind, h, ot_buf, ot_ap, ntile):
        P = self.P
        nrow = 64 if kind == "b" else 65
        pt = self.ps((6, 7))
        for i in range(ntile):
            P.tr(pt, pt[:, 128 * i:128 * i + nrow], ot_buf, ot_ap[0:nrow, 128 * i:128 * (i + 1)], self.identf, self.identf[0:nrow, 0:nrow])
        for i in range(ntile):
            o = self.otok[i] if kind == "b" else self.otokc[i]
            if kind == "b":
                P.op("dve", lambda e, o=o, pt=pt, i=i, h=h: e.tensor_copy(out=o[:, 64 * h:64 * h + 64], in_=pt[:, 128 * i:128 * i + 64]),
                     r=(pt,), w=(o,))
            else:
                rc = self.small()
                P.op("dve", lambda e, rc=rc, pt=pt, i=i: e.reciprocal(out=rc[:, 0:1], in_=pt[:, 128 * i + 64:128 * i + 65]), r=(pt,), w=(rc,))
                P.op("dve", lambda e, o=o, pt=pt, i=i, h=h, rc=rc: e.tensor_scalar(out=o[:, 64 * h:64 * h + 64], in0=pt[:, 128 * i:128 * i + 64],
                                                                               scalar1=rc[:, 0:1], scalar2=None, op0=ALU.mult),
                     r=(pt, rc), w=(o,))

    def attn_prompt(self, g):
        P = self.P
        for h in range(6):
            psO = {"b": self.psb[4], "c": self.psb[5]}
            qT = {"b": self.qbT[h // 2], "c": self.qcT[h // 2]}
            units = {"b": [], "c": []}
            for kind in ("b", "c"):
                for kb in range(GT * g + GT - 1, -1, -1):
                    qlo = max(0, kb - GT * g) * 128
                    m = None
                    if kb >= GT * g:
                        mb = self.mstrict if kind == "b" else self.mincl
                        m = (mb, mb[:], 128)
                    units[kind].append((kb, qlo, m, True))
                self.attn_begin(kind, 512, psO[kind])
            st = {}
            nu = len(units["b"])
            cur = {kind: self.attn_s1(kind, h, qT[kind], 0, 512, units[kind][0]) for kind in ("b", "c")}
            for k in range(nu):
                nxt_ = {}
                if k + 1 < nu:
                    nxt_ = {kind: self.attn_s1(kind, h, qT[kind], 0, 512, units[kind][k + 1]) for kind in ("b", "c")}
                for kind in ("b", "c"):
                    self.attn_s2(cur[kind], psO[kind], k == 0, k == nu - 1, st)
                cur = nxt_
            for kind in ("b", "c"):
                nrow = 64 if kind == "b" else 65
                ot = self.nxt("OT", self.OT)
                po = psO[kind]
                P.op("dve", lambda e, po=po, ot=ot, nrow=nrow: e.tensor_copy(out=ot[0:nrow, :], in_=po[0:nrow, :]), r=(po,), w=(ot,))
                self.o_to_tok(kind, h, ot, ot[:], GT)
        for kind in ("b", "c"):
            for i in range(GT):
                ok = self.otok[i] if kind == "b" else self.otokc[i]
                self.norm_into_cat(ok, ok[:, 0:384], 384, i, 256 if kind == "b" else 640)

    def sample_setup(self):
        P = self.P
        nc = self.nc
        dt = lambda n, s_, d, k="ExternalInput": nc.dram_tensor(n, list(s_), d, kind=k).ap()
        self.xs_d = dt("xs", [128, D], F32)
        self.pt_d = dt("pt", [1, 256], I32)
        R = L * NPOOL * 128
        self.c_all = dt("c_all", [R, 1542], F32)
        self.y_s = dt("y_s", [128, D], F32, "ExternalOutput")
        self.s_sbk = dt("s_sbk", [L, 128, 384], F32, "ExternalOutput")
        self.s_sbv = dt("s_sbv", [L, 128, 384], F32, "ExternalOutput")
        self.s_fk = dt("s_fk", [L, 128, 384], F32, "ExternalOutput")
        self.s_fv = dt("s_fv", [L, 128, 384], F32, "ExternalOutput")
        self.s_lf = dt("s_lf", [L, 128, 6], F32, "ExternalOutput")
        self.s_cv = dt("s_cv", [L, 128, 256], F32, "ExternalOutput")
        sb = P.sb
        self.xsr = sb("xsr", [128, D], F32)
        P.dma("sp", self.xsr, self.xsr[:], None, self.xs_d)
        self.mblk_s = sb("mblk_s", [128, 128], BF16)
        self.mblk_i = sb("mblk_i", [128, 128], BF16)
        self.bcum = sb("bcum", [128, 128], F32)
        P.dma("sp", self.mblk_s, self.mblk_s[:], None, self.cd["c_mblk_s"])
        P.dma("sp", self.mblk_i, self.mblk_i[:], None, self.cd["c_mblk_i"])
        P.dma("sp", self.bcum, self.bcum[:], None, self.cd["c_bcum"])
        ptb = sb("ptb", [128, 256], I32)
        iot = sb("iot", [128, 1], I32)
        iotf = sb("iotf", [128, 1], F32)
        P.dma("sp", ptb, ptb[:], None, self.pt_d[0, :].partition_broadcast(128))
        P.op("pool", lambda e: e.iota(iot[:], pattern=[[0, 1]], base=0, channel_multiplier=1), w=(iot,))
        P.op("dve", lambda e: e.tensor_copy(out=iotf[:], in_=iot[:]), r=(iot,), w=(iotf,))
        self.idx = []
        for l in range(L):
            ix = sb(f"idx{l}", [128, 256], I32)
            io2 = sb(f"iotf{l}", [128, 1], F32)
            P.op("dve", lambda e, io2=io2, l=l: e.tensor_scalar(out=io2[:], in0=iotf[:], scalar1=float(l * NPOOL * 128), scalar2=None, op0=ALU.add),
                 r=(iotf,), w=(io2,))
            P.op("dve", lambda e, ix=ix, io2=io2: e.tensor_scalar(out=ix[:], in0=ptb[:], scalar1=128.0, scalar2=io2[:, 0:1], op0=ALU.mult, op1=ALU.add),
                 r=(ptb, io2), w=(ix,))
            self.idx.append(ix)
        self.QB = [sb(f"QB{p}", [128, 256], BF16) for p in range(3)]
        self.QC = [sb(f"QC{p}", [128, 256], BF16) for p in range(3)]
        for q in self.QB + self.QC:
            P.op("pool", lambda e, q=q: e.memset(q[:], 0.0), w=(q,))
        self.m6 = {"b": sb("m6b", [128, 48], BF16), "c": sb("m6c", [128, 48], BF16)}
        self.gc = sb("gc", [128, 6], F32)
        self.ugt = sb("ugt", [128, 128], F32)
        P.dma("sp", self.ugt, self.ugt[:], None, self.cd["c_ugt"])
        self.wsTs = [sb(f"wsTs{g}", [128, 128], BF16) for g in range(4)]
        self.bs_s = sb("bs_s", [128, 4, 64], F32)
        self.bs_col_s = sb("bs_col_s", [128, 4], F32)

    def sample_params(self, l):
        P = self.P
        for i in range(16):
            P.dma("sp", self.bs_col_s, self.bs_col_s[8 * i:8 * i + 8, :], None, self.b_s[l][:, 0:8].rearrange("g t -> t g"),
                  allow_slow_non_contiguous=True)
        P.op("dve", lambda e: e.tensor_copy(out=self.bs_s[:], in_=self.bs_col_s[:, :].unsqueeze(2).to_broadcast([128, 4, 64])),
             r=(self.bs_col_s,), w=(self.bs_s,))
        for g in range(4):
            P.op("pool", lambda e: e.memset(self.wstmp[:], 0.0), w=(self.wstmp,))
            for i in range(16):
                P.dma("sp", self.wstmp, self.wstmp[8 * i:8 * i + 8, 8 * i:8 * i + 8], None, self.w_s[l, g, 0:8, 0:8])
            P.op("dve", lambda e: e.tensor_tensor(out=self.wstmp[:], in0=self.wstmp[:], in1=self.tril[:], op=ALU.mult),
                 r=(self.wstmp, self.tril), w=(self.wstmp,))
            ps = self.ps()
            P.tr(ps, ps[:, 0:128], self.wstmp, self.wstmp[:], self.identf, self.identf[:])
            P.op("dve", lambda e, g=g, ps=ps: e.tensor_copy(out=self.wsTs[g][:], in_=ps[:, 0:128]), r=(ps,), w=(self.wsTs[g],))

    def gather_page(self, l, i, j):
        P = self.P
        ix = self.idx[l]
        off = bass.IndirectOffsetOnAxis(ap=ix[:, 16 * i + j:16 * i + j + 1], axis=0)
        st = self.gstag[self.gn % len(self.gstag)]
        self.gn += 1
        P.op("pool", lambda e: e.indirect_dma_start(out=st[:, 0:1542], out_offset=None, in_=self.c_all, in_offset=off),
             r=(ix,), w=(st,), dma=True)
        vb_, vc_ = self.vb[j], self.vc[j]
        P.op("act", lambda e: e.activation(out=vb_[:], in_=st[:, 384:768], func=AF.Copy), r=(st,), w=(vb_,))
        P.op("act", lambda e: e.activation(out=vc_[:, :, 0:64], in_=st[:, 1152:1536].rearrange("p (h d) -> p h d", h=6), func=AF.Copy),
             r=(st,), w=(vc_,))
        for kind, c0 in (("b", 0), ("c", 768)):
            kb16 = self.nxt("xs16", self.xs16)
            P.op("dve", lambda e, kb16=kb16, c0=c0: e.tensor_copy(out=kb16[:, 0:384], in_=st[:, c0:c0 + 384]), r=(st,), w=(kb16,))
            ps = self.ps((6, 7))
            pb = ps[:].bitcast(BF16)
            for p in range(3):
                P.tr(ps, pb[:, 128 * p:128 * (p + 1)], kb16, kb16[:, 128 * p:128 * (p + 1)], self.ident, self.ident[:])
            for p in range(3):
                dst = self.kT_dst(kind, p, j // GT)
                cc = 128 * (j % GT)
                P.op("dve", lambda e, dst=dst, pb=pb, p=p, cc=cc: e.tensor_copy(out=dst[:, cc:cc + 128], in_=pb[:, 128 * p:128 * (p + 1)]),
                     r=(ps,), w=(dst,))
        ps2 = self.ps((6, 7))
        P.mm(ps2, ps2[:, 0:6], self.ugt, self.ugt[:], st, st[:, 1536:1542])
        P.mm(ps2, ps2[:, 8:14], self.onesf, self.onesf[:], st, st[:, 1536:1542])
        G = self.negF[j]
        if j == 15:
            P.op("dve", lambda e: e.tensor_copy(out=G[:], in_=ps2[:, 0:6]), r=(ps2,), w=(G,))
            P.op("dve", lambda e: e.tensor_copy(out=self.gc[:], in_=ps2[:, 8:14]), r=(ps2,), w=(self.gc,))
        else:
            P.op("dve", lambda e: e.tensor_tensor(out=G[:], in0=ps2[:, 0:6], in1=self.gc[:], op=ALU.add), r=(ps2, self.gc), w=(G,))
            P.op("dve", lambda e: e.tensor_tensor(out=self.gc[:], in0=ps2[:, 8:14], in1=self.gc[:], op=ALU.add), r=(ps2, self.gc), w=(self.gc,))

    def s_s1(self, kind, i, kb, mask6):
        P = self.P
        Q = self.QB if kind == "b" else self.QC

        def zmm(ps, final_stop):
            mms = []
            for p in range(3):
                kT = self.kT_dst(kind, p, kb // GT)
                kc0 = 128 * (kb % GT)
                mms.append((ps[:, 16 * p:16 * p + 16], kT, kT[:, kc0:kc0 + 128], Q[p], Q[p][:, 16 * i:16 * i + 16]))
            if mask6 is not None:
                mms.append((ps[:, 0:48], self.ident, self.ident[:], mask6, mask6[:]))
            for n_, (o_ap, lb, l_ap, rb, r_ap) in enumerate(mms):
                P.mm(ps, o_ap, lb, l_ap, rb, r_ap, start=(n_ == 0), stop=(final_stop and n_ == len(mms) - 1))

        c = {"kind": kind, "kb": kb, "zmm": zmm}
        if kind == "b":
            psZ = self.ps((0, 1))
            zmm(psZ, True)
            e_ = self.nxt("ebuf", self.ebuf)
            l_ = self.nxt("lbuf", self.lbuf)
            P.op("act", lambda e: e.activation(out=e_[:, 0:48], in_=psZ[:, 0:48], func=AF.Exp), r=(psZ,), w=(e_,))
            P.op("act", lambda e: e.activation(out=l_[:, 0:48], in_=e_[:, 0:48], func=AF.Ln, bias=self.one[:], scale=1.0),
                 r=(e_, self.one), w=(l_,))
            c["l"] = l_
        else:
            psZ = self.ps((6, 7))
            zmm(psZ, True)
            tmp = self.nxt("t512", self.t512)
            G = self.negF[kb]
            P.op("dve", lambda e: e.tensor_tensor(out=tmp[:, 0:48].rearrange("p (h t) -> p h t", h=6), in0=psZ[:, 0:48].rearrange("p (h t) -> p h t", h=6),
                                                 in1=G[:, :].unsqueeze(2).to_broadcast([128, 6, 8]), op=ALU.add), r=(psZ, G), w=(tmp,))
            a_ = self.nxt("abufc", self.abufc)
            P.op("act", lambda e: e.activation(out=a_[:, 0:48], in_=tmp[:, 0:48], func=AF.Exp), r=(tmp,), w=(a_,))
            c["a"] = a_
        return c

    def s_s2(self, c, psO, first, last):
        P = self.P
        kind, kb = c["kind"], c["kb"]
        if kind == "b":
            l_ = c["l"]
            psE = self.ps((2, 3))
            c["zmm"](psE, False)
            P.mm(psE, psE[:, 0:48], self.negU, self.negU[:], l_, l_[:, 0:48], start=False, stop=first)
            if not first:
                ab_ = self.accb_cur
                P.mm(psE, psE[:, 0:48], self.negones, self.negones[:], ab_, ab_[:, 0:48], start=False, stop=True)
            a_ = self.nxt("abuf", self.abuf)
            P.op("act", lambda e: e.activation(out=a_[:, 0:48], in_=psE[:, 0:48], func=AF.Exp), r=(psE,), w=(a_,))
            if not last:
                if first:
                    P.op("dve", lambda e: e.tensor_copy(out=self.acc[:, 0:48], in_=l_[:, 0:48]), r=(l_,), w=(self.acc,))
                else:
                    P.op("dve", lambda e: e.tensor_tensor(out=self.acc[:, 0:48], in0=self.acc[:, 0:48], in1=l_[:, 0:48], op=ALU.add),
                         r=(self.acc, l_), w=(self.acc,))
                nb = self.nxt("accb", self.accb)
                P.op("dve", lambda e: e.tensor_copy(out=nb[:, 0:48], in_=self.acc[:, 0:48]), r=(self.acc,), w=(nb,))
                self.accb_cur = nb
            vt = self.vb[kb]
            for h in range(6):
                P.mm(psO, psO[0:64, 8 * h:8 * h + 8], vt, vt[:, 64 * h:64 * h + 64], a_, a_[:, 8 * h:8 * h + 8], start=False,
                     stop=(last and h == 5))
        else:
            a_ = c["a"]
            vt = self.vc[kb]
            for h in range(6):
                P.mm(psO, psO[0:65, 8 * h:8 * h + 8], vt, vt[:, h, :], a_, a_[:, 8 * h:8 * h + 8], start=False, stop=(last and h == 5))

    def attn_sample(self, l):
        P = self.P
        otb, otc = self.xg[1], self.xg[2]
        self.gn = 0
        self.gp = 0
        for kind, Q, qT in (("b", self.QB, self.qbT), ("c", self.QC, self.qcT)):
            for p in range(3):
                for half in range(2):
                    r0 = 64 * half
                    P.op("dve", lambda e, Q=Q, qT=qT, p=p, r0=r0, half=half: e.tensor_copy(
                        out=Q[p][r0:r0 + 64, :].rearrange("p (i c) -> p i c", c=16)[:, :, 8 * half:8 * half + 8],
                        in_=qT[p][r0:r0 + 64, 0:128].rearrange("p (i t) -> p i t", t=8)), r=(qT[p],), w=(Q[p],))
        for i in range(16):
            m6 = {}
            for kind, mb in (("b", self.mblk_s), ("c", self.mblk_i)):
                m = self.m6[kind]
                P.op("dve", lambda e, m=m, mb=mb, i=i: e.tensor_copy(out=m[:, :].rearrange("p (h t) -> p h t", h=6),
                                                                   in_=mb[:, 8 * i:8 * i + 8].unsqueeze(1).to_broadcast([128, 6, 8])), r=(mb,), w=(m,))
                m6[kind] = m
            psOb = self.psb[4]
            psOc = self.psb[5]
            P.mm(psOb, psOb[0:64, 0:48], self.zt, self.zt[:, 0:64], self.zt, self.zt[:, 0:48], start=True, stop=False)
            P.mm(psOc, psOc[0:65, 0:48], self.zt, self.zt[:, 0:65], self.zt, self.zt[:, 0:48], start=True, stop=False)
            cur = [self.s_s1("b", i, NT, m6["b"]), self.s_s1("c", i, NT, m6["c"])]
            first = True
            for j in range(15, -2, -1):
                nxt_ = None
                if j >= 0:
                    pos = 16 * i + (15 - j)
                    while self.gp < min(pos + 3, 256):
                        gi, gj = divmod(self.gp, 16)
                        self.gather_page(l, gi, 15 - gj)
                        self.gp += 1
                    nxt_ = [self.s_s1("b", i, j, None), self.s_s1("c", i, j, None)]
                self.s_s2(cur[0], psOb, first, j < 0)
                self.s_s2(cur[1], psOc, first, j < 0)
                first = False
                cur = nxt_
            P.op("dve", lambda e, i=i: e.tensor_copy(out=otb[0:64, 0:768].rearrange("p (h c) -> p h c", h=6)[:, :, 8 * i:8 * i + 8],
                                                    in_=psOb[0:64, 0:48].rearrange("p (h t) -> p h t", h=6)), r=(psOb,), w=(otb,))
            P.op("dve", lambda e, i=i: e.tensor_copy(out=otc[0:65, 0:768].rearrange("p (h c) -> p h c", h=6)[:, :, 8 * i:8 * i + 8],
                                                    in_=psOc[0:65, 0:48].rearrange("p (h t) -> p h t", h=6)), r=(psOc,), w=(otc,))
        P.handoff(self.gstag, self.otokc)
        for kind in ("b", "c"):
            ot = otb if kind == "b" else otc
            for h in range(6):
                self.o_to_tok(kind, h, ot, ot[:, 128 * h:128 * (h + 1)], 1)
            ok = self.otok[0] if kind == "b" else self.otokc[0]
            self.norm_into_cat(ok, ok[:, 0:384], 384, 0, 256 if kind == "b" else 640)

    def sample_layer(self, l):
        P = self.P
        self.prompt_mode = False
        self.wsT_cur = self.wsTs
        self.bs_cur = self.bs_s
        self.sample_params(l)
        self.norm_to_T(self.xsr, "attn", self.xnT, 0)
        outs = {"sbk": self.s_sbk[l], "sbv": self.s_sbv[l], "fk": self.s_fk[l], "fv": self.s_fv[l], "cv": self.s_cv[l]}
        self.inproj(l, NG, [NT], 128, outs)
        P.dma("sp", None, self.s_lf[l], self.lfo, self.lfo[:, 0, :])
        P.handoff([self.hT], self.hviews)
        self.attn_sample(l)
        P.handoff(self.hviews, [self.hT])
        self.merge_wo(l, 1, 128, [self.xsr])
        self.ffn(l, 1, 128, [self.xsr])
        if l == L - 1:
            self.final_norm(self.xsr, self.y_s)

    def merge_wo(self, l, nt, ntok, xbufs):
        P = self.P
        for i in range(nt):
            self.to_T(self.cat[i], "mix", self.xnT, 128 * i)
        for cc in range(2):
            wb = self.load_w(self.w_o[l][:, 512 * cc:512 * (cc + 1)], 512)
            for i in range(nt):
                ps = self.ps()
                self.mm_tok(ps, 512, self.xnT, 128 * i, wb, 0)
                xb = xbufs[i]
                P.op("dve", lambda e, xb=xb, ps=ps, cc=cc: e.tensor_tensor(out=xb[:, 512 * cc:512 * (cc + 1)], in0=xb[:, 512 * cc:512 * (cc + 1)],
                                                                            in1=ps[:], op=ALU.add), r=(xb, ps), w=(xb,))

    def ffn(self, l, nt, ntok, xbufs):
        P = self.P
        for i in range(nt):
            self.norm_to_T(xbufs[i], "ffn", self.xnT, 128 * i)
        AB = (0, 1, 2, 3, 4, 5, 6, 7)
        P.handoff(self.wbuf, self.wh)
        for jb in range(NJ // 2):
            ws = []
            for c0 in (256 * jb, DFF + 256 * jb):
                wb = self.nxt("wh", self.wh)
                P.dma("pool", wb, wb[:, :, :], None, self.w_fi[l][:, c0:c0 + 256].rearrange("(c p) n -> p c n", p=128))
                ws.append(wb)
            wg, wu = ws
            for jj in range(2):
                j = 2 * jb + jj
                pg = self.ps(AB)
                self.mm_feat(pg, 128, wg, 128 * jj, self.xnT, ntok)
                pu = self.ps(AB)
                self.mm_feat(pu, 128, wu, 128 * jj, self.xnT, ntok)
                sg = self.nxt("t512", self.t512)
                P.op("act", lambda e, pg=pg, sg=sg: e.activation(out=sg[:, 0:ntok], in_=pg[:, 0:ntok], func=AF.Silu), r=(pg,), w=(sg,))
                P.op("dve", lambda e, pu=pu, sg=sg, j=j: e.tensor_tensor(out=self.hT[:, j, 0:ntok], in0=sg[:, 0:ntok], in1=pu[:, 0:ntok], op=ALU.mult),
                     r=(sg, pu), w=(self.hT,))
        P.handoff(self.wh, self.wbuf)
        slots = self.wob + self.cat
        for j in range(NJ):
            wo = self.nxt("wob", slots)
            P.dma("pool", wo, wo[:], None, self.w_fo[l][128 * j:128 * (j + 1), :])
            for i in range(nt):
                for cc in range(2):
                    ps = self.psb[2 * i + cc]
                    P.mm(ps, ps[:], self.hT, self.hT[:, j, 128 * i:128 * (i + 1)], wo, wo[:, 512 * cc:512 * (cc + 1)],
                         start=(j == 0), stop=(j == NJ - 1))
        for i in range(nt):
            for cc in range(2):
                ps = self.psb[2 * i + cc]
                xb = xbufs[i]
                P.op("dve", lambda e, xb=xb, ps=ps, cc=cc: e.tensor_tensor(out=xb[:, 512 * cc:512 * (cc + 1)], in0=xb[:, 512 * cc:512 * (cc + 1)],
                                                                            in1=ps[:], op=ALU.add), r=(xb, ps), w=(xb,))

    def final_norm(self, xb, dram_ap):
        P = self.P
        if self.gfin is None:
            self.gfin = P.sb("gfin", [128, D], F32)
            P.dma("sp", self.gfin, self.gfin[:], None, self.g_final.partition_broadcast(128))
        sm = self.rms_stats(xb, xb[:], D)
        P.op("dve", lambda e: e.scalar_tensor_tensor(out=xb[:], in0=xb[:], scalar=sm[:, 0:1], in1=self.gfin[:], op0=ALU.mult, op1=ALU.mult),
             r=(xb, sm, self.gfin), w=(xb,))
        P.dma("sp", None, dram_ap, xb, xb[:])

    def prompt_group(self, l, g):
        P = self.P
        self.prompt_mode = True
        self.wsT_cur = self.wsT
        self.bs_cur = self.bs_b
        tiles = [GT * g + i for i in range(GT)]
        src = self.xp if l == 0 else self.xscr
        for i, t in enumerate(tiles):
            P.dma("sp", self.xg[i], self.xg[i][:], None if l == 0 else self.xscr_b[t], src[128 * t:128 * (t + 1), :])
        for i in range(GT):
            self.norm_to_T(self.xg[i], "attn", self.xnT, 128 * i)
        r0, r1 = 512 * g, 512 * (g + 1)
        outs = {"sbk": self.o_sbk[l, r0:r1, :], "sbv": self.o_sbv[l, r0:r1, :], "fk": self.o_fk[l, r0:r1, :],
                "fv": self.o_fv[l, r0:r1, :], "cv": None}
        import os
        stage = int(os.environ.get("STAGE", "9"))
        if stage >= 1:
            self.inproj(l, g, tiles, 512, outs)
            P.dma("sp", None, self.o_lf[l, r0:r1, :].rearrange("(i p) h -> p i h", p=128), self.lfo, self.lfo[:])
        if stage >= 2:
            P.handoff([self.hT], self.hviews)
            self.attn_prompt(g)
            P.handoff(self.hviews, [self.hT])
        if stage >= 3:
            self.merge_wo(l, GT, 512, self.xg)
        if stage >= 4:
            self.ffn(l, GT, 512, self.xg)
        for i, t in enumerate(tiles):
            if l == L - 1:
                self.final_norm(self.xg[i], self.y_p[128 * t:128 * (t + 1), :])
            else:
                P.dma("sp", self.xscr_b[t], self.xscr[128 * t:128 * (t + 1), :], self.xg[i], self.xg[i][:])

    def build(self):
        import os
        nl = int(os.environ.get("NLAYER", L))
        ng = int(os.environ.get("NGROUP", NG))
        if self.with_sample:
            self.sample_setup()
        for l in range(nl):
            self.layer_params(l)
            for g in range(ng):
                self.prompt_group(l, g)
            if self.with_sample:
                self.sample_layer(l)
        print("sbuf left", self.nc.sbuf_bytes_remaining)
        print("ops", {e: len(self.P.ops[e]) for e in ENGS}, "dma", self.P.ndma)
        self.P.finalize()


_CACHE = {}


def build_nc(with_sample):
    nc = bass.Bass("TRN2", target_bir_lowering=False)
    with ExitStack() as st:
        k = K(nc, st, with_sample)
        k.build()
    return nc


def kernel(**inp):
    f32 = lambda a: np.ascontiguousarray(np.asarray(a), dtype=np.float32)
    nc = build_nc(True)
    consts = host_consts()
    shared = {"w_in": f32(inp["w_in"]), "w_o": f32(inp["w_o"]), "w_ffn_in": f32(inp["w_ffn_in"]),
              "w_ffn_out": f32(inp["w_ffn_out"]), "g_attn": f32(inp["g_attn"]), "g_mix": f32(inp["g_mix"]),
              "g_ffn": f32(inp["g_ffn"]), "g_final": f32(inp["g_final"]), "g_v": f32(inp["g_v"]), "b_f": f32(inp["b_f"]),
              "w_s": f32(inp["w_s"]), "b_s": f32(inp["b_s"])}
    shared.update(consts)
    c_all = np.empty((L * NPOOL * 128, 1542), np.float32)
    for k0, nm in ((0, "cache_sb_k"), (384, "cache_sb_v"), (768, "cache_fox_k"), (1152, "cache_fox_v")):
        c_all[:, k0:k0 + 384] = np.asarray(inp[nm], dtype=np.float32).reshape(-1, 384)
    c_all[:, 1536:1542] = np.asarray(inp["cache_fox_logf"], dtype=np.float32).reshape(-1, 6)
    shared["c_all"] = c_all
    xp = f32(inp["x_prompt"])
    xs = f32(inp["x_sample"])
    pt = np.ascontiguousarray(np.asarray(inp["page_table"]), dtype=np.int32)
    in_maps = []
    for c in range(8):
        m = dict(shared)
        m["xp"] = xp[c]
        m["xs"] = xs[16 * c:16 * (c + 1)].reshape(128, D)
        m["pt"] = pt[16 * c:16 * (c + 1)].reshape(1, 256)
        in_maps.append(m)
    res = run_bass_kernel_spmd(nc, in_maps, core_ids=list(range(8))).results
    y_p = np.stack([r["y_p"] for r in res])

    def pk(n):
        return np.stack([r[n] for r in res], axis=1).reshape(L, 8, S, 6, 64)

    def sk(n, tail):
        return np.concatenate([r[n].reshape(L, 16, 8, *tail) for r in res], axis=1)

    p_lf = np.stack([r["o_lf"] for r in res], axis=1)
    y_s = np.concatenate([r["y_s"].reshape(16, 8, D) for r in res], axis=0)
    return (y_p, y_s, pk("o_sbk"), pk("o_sbv"), pk("o_fk"), pk("o_fv"), p_lf,
            sk("s_sbk", (6, 64)), sk("s_sbv", (6, 64)), sk("s_fk", (6, 64)), sk("s_fv", (6, 64)), sk("s_lf", (6,)),
            sk("s_cv", (256,)))
```
